# Optimizing a Trainium2 kernel written in Bass

```python
import jax, jax.numpy as jnp
from jax import lax
import numpy as np

D_MODEL = 1024
BATCH = 16
SEQ = 2048
DEPTH = 1

CHUNK = 64
Q_BLOCK = 128
HEAD_DIM = 64
D_MIX = D_MODEL
SB_HEADS = 8
SB_WIDTH = SB_HEADS * HEAD_DIM
RW_WIDTH = D_MIX - SB_WIDTH
RW_HEADS = RW_WIDTH // HEAD_DIM
DECAY_LORA = 64
AAA_LORA = 64
GATE_LORA = 128
RW_PROJ = 3 * RW_WIDTH + DECAY_LORA + AAA_LORA + GATE_LORA
PROJ_WIDTH = 3 * SB_WIDTH + RW_PROJ
D_FF = 2816
FFN_HALF = 0.5
RMS_EPS = 1e-6
LNX_EPS = 64e-5

kernel_name = "hybrid_stickbreak_rwkv7_macaron"


def rms_norm(x, g, eps=RMS_EPS):
    xf = x.astype(jnp.float32)
    ms = jnp.mean(xf * xf, axis=-1, keepdims=True)
    return (xf * lax.rsqrt(ms + eps) * g.astype(jnp.float32)).astype(x.dtype)


def swiglu(x, w_gate, w_up, w_down):
    return (jax.nn.silu(x @ w_gate) * (x @ w_up)) @ w_down


def token_shift(u):
    return jnp.pad(u[:, :-1], ((0, 0), (1, 0), (0, 0)))


def stick_breaking_attention(q, k, v):
    S, Dh = q.shape[1], q.shape[-1]
    scale = Dh ** -0.5
    outs = []
    for blk in range(S // Q_BLOCK):
        q0 = blk * Q_BLOCK
        kv_len = q0 + Q_BLOCK
        qb, kb, vb = q[:, q0:kv_len], k[:, :kv_len], v[:, :kv_len]
        z = jnp.einsum('bqhd,bkhd->bhqk', qb, kb).astype(jnp.float32) * scale
        q_pos = q0 + jnp.arange(Q_BLOCK)[:, None]
        k_pos = jnp.arange(kv_len)[None, :]
        mask = k_pos < q_pos
        log_beta = jax.nn.log_sigmoid(z)
        log_1m_beta = jnp.where(mask, jax.nn.log_sigmoid(-z), 0.0)
        log_remain = lax.cumsum(log_1m_beta, axis=3, reverse=True) - log_1m_beta
        att = jnp.where(mask, jnp.exp(log_beta + log_remain), 0.0)
        outs.append(jnp.einsum('bhqk,bkhd->bqhd', att.astype(vb.dtype), vb))
    return jnp.concatenate(outs, axis=1)


def rwkv7_recurrence(r, decay, k, v, a_vec, b_vec):
    B, S, H, N = r.shape
    n_chunks = S // CHUNK

    def to_chunks(t):
        return t.astype(jnp.float32).transpose(1, 0, 2, 3).reshape(n_chunks, CHUNK, B, H, N)

    xs = (to_chunks(r), to_chunks(decay), to_chunks(k), to_chunks(v),
          to_chunks(a_vec), to_chunks(b_vec))

    def step(state, inp):
        r_t, w_t, k_t, v_t, a_t, b_t = inp
        sa = jnp.einsum('bhvk,bhk->bhv', state, a_t)
        state = (state * w_t[:, :, None, :] + sa[..., None] * b_t[:, :, None, :]
                 + v_t[..., None] * k_t[:, :, None, :])
        return state, jnp.einsum('bhvk,bhk->bhv', state, r_t)

    def chunk_step(state, chunk_xs):
        return lax.scan(step, state, chunk_xs)

    state0 = jnp.zeros((B, H, N, N), jnp.float32)
    _, y = lax.scan(chunk_step, state0, xs)
    return y.reshape(S, B, H, N).transpose(1, 0, 2, 3)


def rwkv7_time_mix(u, mu, w0, w2, a0, a2, g2, k_k, k_a, r_k, ln_w, ln_b):
    B, S, _ = u.shape
    u = u + (token_shift(u) - u) * mu
    x_r, x_k, x_v, x_w, x_a, x_g = jnp.split(
        u, [RW_WIDTH, 2 * RW_WIDTH, 3 * RW_WIDTH, 3 * RW_WIDTH + DECAY_LORA,
            3 * RW_WIDTH + DECAY_LORA + AAA_LORA], axis=-1)
    log_w = -jax.nn.softplus(-(w0 + jnp.tanh(x_w) @ w2)) - 0.5
    decay = jnp.exp(-jnp.exp(log_w.astype(jnp.float32)))
    a = jax.nn.sigmoid(a0 + x_a @ a2).astype(jnp.float32)
    g = jax.nn.sigmoid(x_g) @ g2

    def heads(t):
        return t.reshape(B, S, RW_HEADS, HEAD_DIM)

    r = heads(x_r).astype(jnp.float32)
    v = heads(x_v).astype(jnp.float32)
    kk = heads(x_k * k_k).astype(jnp.float32)
    kk = kk / jnp.maximum(jnp.linalg.norm(kk, axis=-1, keepdims=True), 1e-12)
    kf = heads(x_k * (1.0 + (a - 1.0) * k_a)).astype(jnp.float32)
    a = heads(a)
    y = rwkv7_recurrence(r, heads(decay), kf, v, -kk, kk * a)
    mean = jnp.mean(y, axis=-1, keepdims=True)
    var = jnp.mean(jnp.square(y - mean), axis=-1, keepdims=True)
    yn = (y - mean) * lax.rsqrt(var + LNX_EPS)
    yn = yn * ln_w.astype(jnp.float32).reshape(RW_HEADS, HEAD_DIM) + ln_b.astype(jnp.float32).reshape(RW_HEADS, HEAD_DIM)
    bonus = jnp.sum(r * kf * r_k.astype(jnp.float32), axis=-1, keepdims=True) * v
    return ((yn + bonus).reshape(B, S, RW_WIDTH) * g.astype(jnp.float32)).astype(u.dtype)


def setup_inputs(seed: int = 0) -> dict:
    key = jax.random.key(seed)
    ks = jax.random.split(key, 26)
    L, D = DEPTH, D_MODEL
    f32 = jnp.float32

    def nrm(k, shape, scale):
        return scale * jax.random.normal(k, shape, f32)

    def gain(k, shape):
        return 1.0 + 0.05 * jax.random.normal(k, shape, f32)

    return {
        "x": jax.random.normal(ks[0], (BATCH, SEQ, D), f32),
        "norm_ffn1": gain(ks[1], (L, D)),
        "ffn1_gate": nrm(ks[2], (L, D, D_FF), D ** -0.5),
        "ffn1_up": nrm(ks[3], (L, D, D_FF), D ** -0.5),
        "ffn1_down": nrm(ks[4], (L, D_FF, D), D_FF ** -0.5),
        "norm_mix": gain(ks[5], (L, D)),
        "w_in": nrm(ks[6], (L, D, PROJ_WIDTH), D ** -0.5),
        "sb_q_norm": gain(ks[7], (L, HEAD_DIM)),
        "sb_k_norm": gain(ks[8], (L, HEAD_DIM)),
        "sb_out_norm": gain(ks[9], (L, SB_WIDTH)),
        "rw_mu": jax.random.uniform(ks[10], (L, RW_PROJ), f32),
        "rw_w0": jax.random.uniform(ks[11], (L, RW_WIDTH), f32, minval=-5.0, maxval=-1.0),
        "rw_w2": nrm(ks[12], (L, DECAY_LORA, RW_WIDTH), 0.5 * DECAY_LORA ** -0.5),
        "rw_a0": nrm(ks[13], (L, RW_WIDTH), 0.1),
        "rw_a2": nrm(ks[14], (L, AAA_LORA, RW_WIDTH), 0.5 * AAA_LORA ** -0.5),
        "rw_g2": nrm(ks[15], (L, GATE_LORA, RW_WIDTH), GATE_LORA ** -0.5),
        "rw_k_k": 0.85 + 0.05 * jax.random.normal(ks[16], (L, RW_WIDTH), f32),
        "rw_k_a": gain(ks[17], (L, RW_WIDTH)),
        "rw_r_k": nrm(ks[18], (L, RW_HEADS, HEAD_DIM), 0.1),
        "rw_ln_w": gain(ks[19], (L, RW_WIDTH)),
        "rw_ln_b": nrm(ks[20], (L, RW_WIDTH), 0.01),
        "w_out": nrm(ks[21], (L, D_MIX, D), D_MIX ** -0.5),
        "norm_ffn2": gain(ks[22], (L, D)),
        "ffn2_gate": nrm(ks[23], (L, D, D_FF), D ** -0.5),
        "ffn2_up": nrm(ks[24], (L, D, D_FF), D ** -0.5),
        "ffn2_down": nrm(ks[25], (L, D_FF, D), D_FF ** -0.5),
    }


def reference(x, norm_ffn1, ffn1_gate, ffn1_up, ffn1_down, norm_mix, w_in,
              sb_q_norm, sb_k_norm, sb_out_norm, rw_mu, rw_w0, rw_w2, rw_a0, rw_a2,
              rw_g2, rw_k_k, rw_k_a, rw_r_k, rw_ln_w, rw_ln_b, w_out, norm_ffn2,
              ffn2_gate, ffn2_up, ffn2_down):
    B, S, _ = x.shape
    for l in range(DEPTH):
        x = x + FFN_HALF * swiglu(rms_norm(x, norm_ffn1[l]), ffn1_gate[l], ffn1_up[l], ffn1_down[l])

        h = rms_norm(x, norm_mix[l])
        p = h @ w_in[l]
        sb_q, sb_k, sb_v, rw_in = jnp.split(p, [SB_WIDTH, 2 * SB_WIDTH, 3 * SB_WIDTH], axis=-1)

        q = rms_norm(sb_q.reshape(B, S, SB_HEADS, HEAD_DIM), sb_q_norm[l])
        k = rms_norm(sb_k.reshape(B, S, SB_HEADS, HEAD_DIM), sb_k_norm[l])
        v = sb_v.reshape(B, S, SB_HEADS, HEAD_DIM)
        o_sb = stick_breaking_attention(q, k, v)
        o_sb = rms_norm(o_sb, sb_out_norm[l].reshape(SB_HEADS, HEAD_DIM)).reshape(B, S, SB_WIDTH)

        o_rw = rwkv7_time_mix(rw_in, rw_mu[l], rw_w0[l], rw_w2[l], rw_a0[l], rw_a2[l], rw_g2[l],
                              rw_k_k[l], rw_k_a[l], rw_r_k[l], rw_ln_w[l], rw_ln_b[l])

        x = x + jnp.concatenate([o_sb, o_rw], axis=-1) @ w_out[l]

        x = x + FFN_HALF * swiglu(rms_norm(x, norm_ffn2[l]), ffn2_gate[l], ffn2_up[l], ffn2_down[l])
    return x
```

```python
import os
from contextlib import ExitStack
import numpy as np
import concourse.bass as bass
import concourse.mybir as mybir
from concourse.bass_utils import run_bass_kernel_spmd

F32 = mybir.dt.float32
BF16 = mybir.dt.bfloat16
AF = mybir.ActivationFunctionType
ALU = mybir.AluOpType
AX = mybir.AxisListType

D = 1024
SEQ = 2048
NTOK = 4096
DFF = 2816
NFF = DFF // 128
PROJ = 3328
RMS_EPS = 1e-6
LNX_EPS = 64e-5

COMPUTE = ("pe", "act", "dve", "pool")
N_DMA_SEMS = 24
DEBUG = False


class Res:
    __slots__ = ("name", "w", "r", "psum")

    def __init__(self, name="", psum=False):
        self.name = name
        self.w = None
        self.r = []
        self.psum = psum


class Op:
    __slots__ = ("eng", "fn", "deps", "is_dma", "sem", "cnt", "signal", "idx", "queue")


class Prog:
    def __init__(self, nc, es):
        self.nc = nc
        self.ops = []
        self.res = []
        self.sems = {e: es.enter_context(nc.semaphore("s_" + e)) for e in COMPUTE}
        self.dsems = [es.enter_context(nc.semaphore("d%d" % i)) for i in range(N_DMA_SEMS)]
        self.cnt = {e: 0 for e in COMPUTE}
        self.dcnt = [0] * N_DMA_SEMS
        self.nd = 0
        self.base = {}

    def R(self, name="", psum=False):
        r = Res(name, psum)
        self.res.append(r)
        return r

    def RP(self):
        return self.R("", True)

    def op(self, eng, fn, reads=(), writes=(), queue=None):
        o = Op()
        o.idx = len(self.ops)
        o.eng = eng
        o.fn = fn
        o.is_dma = eng == "dma"
        o.queue = (queue or "sp") if o.is_dma else eng
        o.signal = False
        deps = set()
        for r in reads:
            if r.w is not None:
                deps.add(r.w)
            if r.psum:
                deps.update(j for j in r.r if self.ops[j].eng != eng)
        for r in writes:
            if r.w is not None:
                deps.add(r.w)
            deps.update(r.r)
        for r in reads:
            r.r.append(o.idx)
        for r in writes:
            r.w = o.idx
            r.r = []
        deps.discard(o.idx)
        o.deps = deps
        self.ops.append(o)
        return o

    def emit(self, final=False):
        nc = self.nc
        ops = self.ops
        for o in ops:
            for d in o.deps:
                od = ops[d]
                if od.eng == "pe" and o.eng == "pe":
                    continue
                od.signal = True
        last = {}
        for o in ops:
            if not o.is_dma:
                last[o.eng] = o
        for o in last.values():
            o.signal = True
        dlast = [None] * N_DMA_SEMS
        for o in ops:
            if o.is_dma:
                k = self.nd % N_DMA_SEMS
                self.nd += 1
                o.sem = k
                if dlast[k] is not None:
                    o.deps.add(dlast[k])
                dlast[k] = o.idx
                self.dcnt[k] += 16
                o.cnt = self.dcnt[k]
                o.signal = True
            elif o.signal:
                self.cnt[o.eng] += 1
                o.cnt = self.cnt[o.eng]
            else:
                o.cnt = None
        streams = {}
        for o in ops:
            streams.setdefault(o.queue, []).append(o)
        sems, dsems = self.sems, self.dsems
        base = dict(self.base)
        endvals = {("c", e): self.cnt[e] for e in COMPUTE}
        endvals.update({("d", k): self.dcnt[k] for k in range(N_DMA_SEMS)})

        def semof(key):
            return dsems[key[1]] if key[0] == "d" else sems[key[1]]

        def run_stream(engobj, qname):
            known = {}
            for key, v in base.items():
                if v > 0 and not (key == ("c", qname)):
                    engobj.wait_ge(semof(key), v)
                known[key] = v
            for o in streams.get(qname, []):
                need = {}
                for d in o.deps:
                    od = ops[d]
                    if od.eng == "pe" and o.eng == "pe":
                        continue
                    key = ("d", od.sem) if od.is_dma else ("c", od.eng)
                    if need.get(key, 0) < od.cnt:
                        need[key] = od.cnt
                for key, v in need.items():
                    if known.get(key, 0) >= v:
                        continue
                    known[key] = v
                    engobj.wait_ge(semof(key), v)
                ins = o.fn(engobj)
                if o.signal:
                    if o.is_dma:
                        ins.then_inc(dsems[o.sem], 16)
                    else:
                        ins.then_inc(sems[o.eng], 1)
            if final and qname == "sp":
                for key, v in endvals.items():
                    if v > known.get(key, 0):
                        engobj.wait_ge(semof(key), v)

        with nc.Block() as block:
            @block.sync
            def _(e):
                run_stream(e, "sp")

            @block.scalar
            def _(e):
                run_stream(e, "act")

            @block.vector
            def _(e):
                run_stream(e, "dve")

            @block.gpsimd
            def _(e):
                run_stream(e, "pool")

            @block.tensor
            def _(e):
                run_stream(e, "pe")
        self.base = endvals
        nops = len(ops)
        self.ops = []
        for r in self.res:
            r.w = None
            r.r = []
        return nops


class Ctx:
    pass


def make_consts(P, nc, es):
    c = Ctx()
    c.ones_bf = es.enter_context(nc.sbuf_tensor("c_ones_bf", [128, 128], BF16))
    c.ident_bf = es.enter_context(nc.sbuf_tensor("c_ident_bf", [128, 128], BF16))
    c.ones_f = es.enter_context(nc.sbuf_tensor("c_ones_f", [128, 128], F32))
    c.ident_f = es.enter_context(nc.sbuf_tensor("c_ident_f", [128, 128], F32))
    return c


def init_consts(P, c):
    r = P.R()
    P.op("pool", lambda e: e.memset(c.ones_bf[:], 1.0), writes=[r])
    P.op("pool", lambda e: e.memset(c.ones_f[:], 1.0), writes=[r])
    P.op("pool", lambda e: e.affine_select(out=c.ident_bf[:], in_=c.ones_bf[:], pattern=[[-1, 128]],
                                           compare_op=ALU.is_equal, fill=0.0, base=0, channel_multiplier=1),
         reads=[r], writes=[r])
    P.op("pool", lambda e: e.affine_select(out=c.ident_f[:], in_=c.ones_f[:], pattern=[[-1, 128]],
                                           compare_op=ALU.is_equal, fill=0.0, base=0, channel_multiplier=1),
         reads=[r], writes=[r])
    c.r = r


def rr(engs):
    i = [0]

    def nxt():
        e = engs[i[0] % len(engs)]
        i[0] += 1
        return e
    return nxt


def scaled_cast(P, eng, out_ap, in_ap, scale_ap, reads, writes):
    if eng == "act":
        if scale_ap is None:
            P.op("act", lambda e: e.copy(out=out_ap, in_=in_ap), reads=reads, writes=writes)
        else:
            P.op("act", lambda e: e.activation(out=out_ap, in_=in_ap, func=AF.Copy, scale=scale_ap), reads=reads, writes=writes)
    else:
        if scale_ap is None:
            P.op(eng, lambda e: e.tensor_copy(out=out_ap, in_=in_ap), reads=reads, writes=writes)
        else:
            P.op(eng, lambda e: e.tensor_scalar(out=out_ap, in0=in_ap, scalar1=scale_ap, scalar2=None, op0=ALU.mult), reads=reads, writes=writes)


def load_gain_cols(P, nc, es, name, g_ap, nk):
    t = es.enter_context(nc.sbuf_tensor(name, [128, nk], F32))
    r = P.R()
    P.op("dma", lambda e: e.dma_start(out=t[:], in_=g_ap.rearrange("(k p) -> p k", p=128), allow_slow_non_contiguous=True), writes=[r])
    return t, r


def ffn_phase(P, nc, C, tag, src, dst, g_ap, wg_ap, wu_ap, wd_ap, ntok=NTOK):
    T = 256
    NS = T // 128
    ntiles = ntok // T
    es = ExitStack()

    def SB(name, shape, dt=F32):
        return es.enter_context(nc.sbuf_tensor(tag + name, shape, dt))

    def PS(name, shape, dt=F32):
        return es.enter_context(nc.psum_tensor(tag + name, shape, dt))

    wg = SB("wg", [128, 8, DFF], BF16)
    wu = SB("wu", [128, 8, DFF], BF16)
    wd = SB("wd", [128, NFF, D], BF16)
    NPC = 8
    PW = DFF // NPC
    r_wg = [P.R() for _ in range(NPC)]
    r_wu = [P.R() for _ in range(NPC)]
    r_wd = [P.R() for _ in range(NFF)]
    NSTG = 3
    stg = [SB("stg%d" % i, [128, DFF], F32) for i in range(NSTG)]
    r_stg = [P.R() for _ in range(NSTG)]
    g_b = SB("g_b", [128, D], F32)
    r_g = P.R()

    xt = [SB("xt%d" % i, [128, NS, D], F32) for i in range(2)]
    r_xt = [P.R() for _ in range(2)]
    hb = SB("hb", [128, NS, D], BF16)
    r_hb = P.R()
    hT = SB("hT", [128, 8, T], BF16)
    r_hT = P.R()
    aT = SB("aT", [128, NFF, T], BF16)
    r_aT = [P.R() for _ in range(NFF)]
    sg = [SB("sg%d" % i, [128, T], F32) for i in range(2)]
    r_sg = [P.R() for _ in range(2)]
    ss = SB("ss", [128, 2 * NS], F32)
    r_ss = P.R()

    ptr = [PS("ptr%d" % i, [128, 1024], BF16) for i in range(2)]
    r_ptr = [P.RP() for _ in range(2)]
    pgu = [PS("pgu%d" % i, [128, 512], F32) for i in range(2)]
    r_pgu = [P.RP() for _ in range(2)]
    pdn = [PS("pdn%d" % i, [128, 512], F32) for i in range(4)]
    r_pdn = [P.RP() for _ in range(4)]

    src_t = src.rearrange("(n s p) d -> n p s d", p=128, s=NS)
    dst_t = dst.rearrange("(n s p) d -> n p s d", p=128, s=NS)

    def load_x(i):
        b = i % 2
        P.op("dma", lambda e: e.dma_start(out=xt[b][:], in_=src_t[i]), writes=[r_xt[b]])

    load_x(0)
    if ntiles > 1:
        load_x(1)

    P.op("dma", lambda e: e.dma_start(out=g_b[:], in_=g_ap.partition_broadcast(128)), writes=[r_g])
    cv = rr(["dve", "act"])
    si = [0]

    def load_w(src_ap3, out_ap3, rres):
        s_ = si[0] % NSTG
        si[0] += 1
        a, b2 = src_ap3.shape[1], src_ap3.shape[2]
        st_ap = stg[s_][:, 0:a * b2].rearrange("p (a b) -> p a b", a=a)
        P.op("dma", lambda e: e.dma_start(out=st_ap, in_=src_ap3), writes=[r_stg[s_]])
        scaled_cast(P, cv(), out_ap3, st_ap, None, [r_stg[s_]], rres)

    wg3 = wg_ap.rearrange("(k p) n -> p k n", p=128)
    wu3 = wu_ap.rearrange("(k p) n -> p k n", p=128)
    ev = rr(["act", "dve"])

    def emit_norm(i):
        b = i % 2
        x, rx = xt[b], r_xt[b]
        for s in range(NS):
            P.op("act", lambda e, s=s, x=x: e.activation(out=hb[:, s, :], in_=x[:, s, :], func=AF.Square, accum_out=ss[:, s:s + 1]),
                 reads=[rx], writes=[r_hb, r_ss])
        P.op("act", lambda e: e.activation(out=ss[:, NS:2 * NS], in_=ss[:, 0:NS], func=AF.Sqrt, bias=RMS_EPS, scale=1.0 / D),
             reads=[r_ss], writes=[r_ss])
        P.op("dve", lambda e: e.reciprocal(out=ss[:, NS:2 * NS], in_=ss[:, NS:2 * NS]), reads=[r_ss], writes=[r_ss])
        for s in range(NS):
            P.op("dve", lambda e, s=s, x=x: e.scalar_tensor_tensor(out=hb[:, s, :], in0=x[:, s, :], scalar=ss[:, NS + s:NS + s + 1], in1=g_b[:], op0=ALU.mult, op1=ALU.mult),
                 reads=[rx, r_ss, r_g], writes=[r_hb])
        for half in range(2):
            pt = ptr[half]
            rp = r_ptr[half]

            def tr(e, half=half, pt=pt):
                ins = None
                for kk in range(4):
                    k = half * 4 + kk
                    for s in range(NS):
                        ins = e.transpose(out=pt[:, kk * T + s * 128: kk * T + (s + 1) * 128], in_=hb[:, s, k * 128:(k + 1) * 128], identity=C.ident_bf[:])
                return ins
            P.op("pe", tr, reads=[r_hb, C.r], writes=[rp])
            scaled_cast(P, ev(), hT[:, half * 4:(half + 1) * 4, :], pt[:, 0:4 * T].rearrange("p (k t) -> p k t", k=4), None, [rp], [r_hT])

    def emit_gu(i, c):
        pb = c % 2
        pg = pgu[pb]

        def gu(e, c=c, pg=pg):
            ins = None
            for k in range(8):
                ins = e.matmul(pg[:, 0:T], lhsT=wg[:, k, c * 128:(c + 1) * 128], rhs=hT[:, k, :], start=(k == 0), stop=(k == 7))
            for k in range(8):
                ins = e.matmul(pg[:, T:2 * T], lhsT=wu[:, k, c * 128:(c + 1) * 128], rhs=hT[:, k, :], start=(k == 0), stop=(k == 7))
            return ins
        pcs = sorted(set([(c * 128) // PW, (c * 128 + 127) // PW]))
        P.op("pe", gu, reads=[r_hT] + [r_wg[j] for j in pcs] + [r_wu[j] for j in pcs], writes=[r_pgu[pb]])
        P.op("act", lambda e, pg=pg, pb=pb: e.activation(out=sg[pb][:], in_=pg[:, 0:T], func=AF.Silu), reads=[r_pgu[pb]], writes=[r_sg[pb]])
        P.op("dve", lambda e, pg=pg, pb=pb, c=c: e.tensor_tensor(out=aT[:, c, :], in0=pg[:, T:2 * T], in1=sg[pb][:], op=ALU.mult),
             reads=[r_pgu[pb], r_sg[pb]], writes=[r_aT[c]])

    def emit_down(i):
        b = i % 2
        x, rx = xt[b], r_xt[b]
        for s in range(NS):
            for h in range(2):
                q = (s * 2 + h) % 4
                pd = pdn[q]

                def dn(e, s=s, h=h, pd=pd):
                    ins = None
                    for c in range(NFF):
                        ins = e.matmul(pd[:], lhsT=aT[:, c, s * 128:(s + 1) * 128], rhs=wd[:, c, h * 512:(h + 1) * 512], start=(c == 0), stop=(c == NFF - 1))
                    return ins
                P.op("pe", dn, reads=r_aT + r_wd, writes=[r_pdn[q]])
                P.op("dve", lambda e, s=s, h=h, pd=pd, x=x: e.scalar_tensor_tensor(out=x[:, s, h * 512:(h + 1) * 512], in0=pd[:], scalar=0.5, in1=x[:, s, h * 512:(h + 1) * 512],
                                                                              op0=ALU.mult, op1=ALU.add),
                     reads=[r_pdn[q], rx], writes=[rx])
        P.op("dma", lambda e, i=i, x=x: e.dma_start(out=dst_t[i], in_=x[:]), reads=[rx])
        if i + 2 < ntiles:
            load_x(i + 2)

    emit_norm(0)
    done_c = 0
    for j in range(NPC):
        load_w(wg3[:, :, j * PW:(j + 1) * PW], wg[:, :, j * PW:(j + 1) * PW], [r_wg[j]])
        load_w(wu3[:, :, j * PW:(j + 1) * PW], wu[:, :, j * PW:(j + 1) * PW], [r_wu[j]])
        while done_c < NFF and (done_c * 128 + 127) // PW <= j:
            emit_gu(0, done_c)
            done_c += 1
    for c in range(0, NFF, 2):
        load_w(wd_ap[c * 128:(c + 2) * 128, :].rearrange("(c p) n -> p c n", p=128), wd[:, c:c + 2, :], [r_wd[c], r_wd[c + 1]])
    emit_down(0)
    for i in range(1, ntiles):
        emit_norm(i)
        for c in range(NFF):
            emit_gu(i, c)
        emit_down(i)
    n = P.emit()
    es.close()
    return n


class TileNorm:
    NPTR = 2

    def __init__(self, P, nc, C, SB, PS, T, nxt=1):
        self.P, self.C, self.T = P, C, T
        NS = T // 128
        self.NS = NS
        self.xt = [SB("xt%d" % i, [128, NS, D], F32) for i in range(nxt)]
        self.r_xt = [P.R() for _ in range(nxt)]
        self.hb = SB("hb", [128, NS, D], BF16)
        self.r_hb = P.R()
        self.hT = SB("hT", [128, 8, T], BF16)
        self.r_hT = P.R()
        self.ss = SB("ss", [128, 2 * NS], F32)
        self.r_ss = P.R()
        nptr = getattr(self, "NPTR", 2) if False else TileNorm.NPTR
        self.ptr = [PS("ptr%d" % i, [128, 1024], BF16) for i in range(nptr)]
        self.r_ptr = [P.RP() for _ in range(nptr)]
        if nptr == 1:
            self.ptr = self.ptr * 2
            self.r_ptr = self.r_ptr * 2
        self.ev = rr(["act", "dve"])

    def load(self, src_t, i):
        b = i % len(self.xt)
        xt, r = self.xt[b], self.r_xt[b]
        self.P.op("dma", lambda e: e.dma_start(out=xt[:], in_=src_t[i]), writes=[r])

    def norm(self, i):
        P, C, T, NS = self.P, self.C, self.T, self.NS
        b = i % len(self.xt)
        x, rx = self.xt[b], self.r_xt[b]
        ss, hb, hT = self.ss, self.hb, self.hT
        for s in range(NS):
            P.op("act", lambda e, s=s: e.activation(out=hb[:, s, :], in_=x[:, s, :], func=AF.Square, accum_out=ss[:, s:s + 1]),
                 reads=[rx], writes=[self.r_hb, self.r_ss])
        P.op("act", lambda e: e.activation(out=ss[:, NS:2 * NS], in_=ss[:, 0:NS], func=AF.Sqrt, bias=RMS_EPS, scale=1.0 / D),
             reads=[self.r_ss], writes=[self.r_ss])
        P.op("dve", lambda e: e.reciprocal(out=ss[:, NS:2 * NS], in_=ss[:, NS:2 * NS]), reads=[self.r_ss], writes=[self.r_ss])
        for s in range(NS):
            P.op("dve", lambda e, s=s: e.tensor_scalar(out=hb[:, s, :], in0=x[:, s, :], scalar1=ss[:, NS + s:NS + s + 1], scalar2=None, op0=ALU.mult),
                 reads=[rx, self.r_ss], writes=[self.r_hb])
        for half in range(2):
            pt, rp = self.ptr[half], self.r_ptr[half]

            def tr(e, half=half, pt=pt):
                ins = None
                for kk in range(4):
                    k = half * 4 + kk
                    for s in range(NS):
                        ins = e.transpose(out=pt[:, kk * T + s * 128: kk * T + (s + 1) * 128], in_=hb[:, s, k * 128:(k + 1) * 128], identity=C.ident_bf[:])
                return ins
            P.op("pe", tr, reads=[self.r_hb, C.r], writes=[rp])
            scaled_cast(P, self.ev(), hT[:, half * 4:(half + 1) * 4, :], pt[:, 0:4 * T].rearrange("p (k t) -> p k t", k=4), None, [rp], [self.r_hT])
        return x, rx


def load_weight_cols(P, nc, SB, name, w_ap, c0, c1, gcol, r_g, piece=832):
    n = c1 - c0
    wt = SB(name, [128, 8, n], BF16)
    rs = [P.R() for _ in range(8)]
    stg = [SB(name + "_stg%d" % i, [128, piece], F32) for i in range(2)]
    r_stg = [P.R() for _ in range(2)]
    cv = rr(["dve", "act"])
    si = 0
    for k in range(8):
        for p0 in range(0, n, piece):
            p1 = min(n, p0 + piece)
            s = si % 2
            si += 1
            st_ap = stg[s][:, 0:p1 - p0]
            P.op("dma", lambda e, st_ap=st_ap, k=k, p0=p0, p1=p1: e.dma_start(out=st_ap, in_=w_ap[k * 128:(k + 1) * 128, c0 + p0:c0 + p1]), writes=[r_stg[s]])
            scaled_cast(P, cv(), wt[:, k, p0:p1], st_ap, None if gcol is None else gcol[:, k:k + 1], [r_stg[s]] + ([r_g] if r_g else []), [rs[k]])
    return wt, rs


def head_norm(P, SB_tmp, ps, rps, gain_b, out_ap, writes, n_heads=8, eps=RMS_EPS, tmp=None):
    junk, r_junk, ssq, r_ssq, t32, r_t32 = tmp
    P.op("act", lambda e: e.activation(out=junk[:], in_=ps, func=AF.Square), reads=rps, writes=[r_junk])
    P.op("dve", lambda e: e.tensor_reduce(out=ssq[:, 0:n_heads], in_=junk[:].rearrange("p (h d) -> p h d", d=64), axis=AX.X, op=ALU.add), reads=[r_junk], writes=[r_ssq])
    P.op("act", lambda e: e.activation(out=ssq[:, 8:8 + n_heads], in_=ssq[:, 0:n_heads], func=AF.Sqrt, bias=eps, scale=1.0 / 64), reads=[r_ssq], writes=[r_ssq])
    P.op("dve", lambda e: e.reciprocal(out=ssq[:, 8:8 + n_heads], in_=ssq[:, 8:8 + n_heads]), reads=[r_ssq], writes=[r_ssq])
    P.op("dve", lambda e: e.tensor_tensor(out=t32[:].rearrange("p (h d) -> p h d", d=64), in0=ps.rearrange("p (h d) -> p h d", d=64),
                                           in1=ssq[:, 8:8 + n_heads].unsqueeze(2).to_broadcast([128, n_heads, 64]), op=ALU.mult),
         reads=rps + [r_ssq], writes=[r_t32])
    P.op("pool", lambda e: e.tensor_tensor(out=out_ap.rearrange("p (h d) -> p h d", d=64), in0=t32[:].rearrange("p (h d) -> p h d", d=64), in1=gain_b, op=ALU.mult),
         reads=[r_t32], writes=writes)


def attn_phase(P, nc, C, tag, src, osb, W, ntok=NTOK):
    T = 256
    NS = 2
    ntiles = ntok // T
    TPS = SEQ // T
    es = ExitStack()

    def SB(name, shape, dt=F32):
        return es.enter_context(nc.sbuf_tensor(tag + name, shape, dt))

    def PS(name, shape, dt=F32):
        return es.enter_context(nc.psum_tensor(tag + name, shape, dt))

    ND = int(os.environ.get("ATTN_ND", "0"))
    WB = int(os.environ.get("ATTN_WB", "0"))
    TileNorm.NPTR = 1 if (ND > 0 or WB > 0) else 2
    TN = TileNorm(P, nc, C, SB, PS, T, nxt=2)
    TileNorm.NPTR = 2
    if ND > 0 or WB > 0:
        pdum = PS("pdum", [128, 512], F32)
        dones = SB("dones", [128, 512], BF16)
        r_dones = P.R()
        P.op("pool", lambda e: e.memset(dones[:], 1.0), writes=[r_dones])
        r_pdum = P.RP()

        def warm(k=ND):
            if k <= 0:
                return

            def f(e):
                ins = None
                for _ in range(k):
                    ins = e.matmul(pdum[:], lhsT=C.ident_bf[:], rhs=dones[:], start=True, stop=True)
                return ins
            P.op("pe", f, reads=[r_dones, C.r], writes=[r_pdum])
    else:
        def warm(k=0):
            pass
    src_t = src.rearrange("(n s p) d -> n p s d", p=128, s=NS)
    osb_t = osb.rearrange("(n s p) d -> n p s d", p=128, s=NS)
    TN.load(src_t, 0)
    gcol, r_g = load_gain_cols(P, nc, es, tag + "gcol", W["norm_mix"], 8)
    wqkv, r_w = load_weight_cols(P, nc, SB, "wqkv", W["w_in"], 0, 1536, gcol, r_g, piece=768)
    r_c = P.R()
    gq_b = SB("gq_b", [128, 64], F32)
    gk_b = SB("gk_b", [128, 64], F32)
    go_b = SB("go_b", [128, 512], F32)
    P.op("dma", lambda e: e.dma_start(out=gq_b[:], in_=W["sb_q_norm"].partition_broadcast(128)), writes=[r_c])
    P.op("dma", lambda e: e.dma_start(out=gk_b[:], in_=W["sb_k_norm"].partition_broadcast(128)), writes=[r_c])
    P.op("dma", lambda e: e.dma_start(out=go_b[:], in_=W["sb_out_norm"].partition_broadcast(128)), writes=[r_c])
    P.op("pool", lambda e: e.tensor_scalar(out=gq_b[:], in0=gq_b[:], scalar1=0.125, scalar2=None, op0=ALU.mult), reads=[r_c], writes=[r_c])
    tri = SB("tri", [128, 128], BF16)
    P.op("pool", lambda e: e.affine_select(out=tri[:], in_=C.ones_bf[:], pattern=[[-1, 128]], compare_op=ALU.is_ge, fill=0.0, base=0, channel_multiplier=1),
         reads=[C.r], writes=[r_c])

    KT = SB("KT", [128, 4, SEQ], BF16)
    r_KT = [P.R() for _ in range(SEQ // 128)]
    Vb = SB("Vb", [128, SEQ // 128, 512], BF16)
    r_Vb = [P.R() for _ in range(SEQ // 128)]
    qT = SB("qT", [128, 4, T], BF16)
    r_qT = P.R()
    qn = [SB("qn%d" % i, [128, 512], BF16) for i in range(2)]
    r_qn = [P.R() for _ in range(2)]
    hn_tmp = (SB("hn_junk", [128, 512], F32), P.R(), SB("hn_ssq", [128, 16], F32), P.R(), SB("hn_t32", [128, 512], F32), P.R())
    ob = SB("ob", [128, NS, 512], BF16)
    r_ob = P.R()

    pz2 = PS("pz2", [128, 1024], F32)
    pproj = [pz2[:, 0:512], pz2[:, 512:1024]]
    r_pproj = [P.RP() for _ in range(2)]
    pcs = [PS("pc%d" % i, [128, 512], F32) for i in range(2)]
    r_pc = [P.RP() for _ in range(2)]
    po = [PS("po%d" % i, [128, 512], F32) for i in range(2)]
    r_po = [P.RP() for _ in range(2)]

    NE = 5
    NL = 4
    NA = 5
    Eb = [SB("E%d" % i, [128, 2, T], F32) for i in range(NE)]
    r_E = [P.R() for _ in range(NE)]
    Lb = [SB("L%d" % i, [128, 2, T], BF16) for i in range(NL)]
    r_L = [P.R() for _ in range(NL)]
    Ls = [SB("Ls%d" % i, [128, 2, T], BF16) for i in range(2)]
    r_Ls = [P.R() for _ in range(2)]
    Xb = [SB("X%d" % i, [128, 2, T], F32) for i in range(2)]
    r_X = [P.R() for _ in range(2)]
    At = [SB("At%d" % i, [128, 2, T], BF16) for i in range(NA)]
    r_At = [P.R() for _ in range(NA)]

    pj = [0]
    for i in range(ntiles):
        g = i % TPS
        q0 = g * T
        x, rx = TN.norm(i)
        if i + 1 < ntiles:
            TN.load(src_t, i + 1)
        hT, r_hT = TN.hT, TN.r_hT
        for which in range(3):
            for s in range(NS):
                b = pj[0] % 2
                pj[0] += 1
                pp, rpp = pproj[b], r_pproj[b]

                def proj(e, which=which, s=s, pp=pp):
                    ins = None
                    for k in range(8):
                        ins = e.matmul(pp, lhsT=hT[:, k, s * 128:(s + 1) * 128], rhs=wqkv[:, k, which * 512:(which + 1) * 512], start=(k == 0), stop=(k == 7))
                    return ins
                P.op("pe", proj, reads=[r_hT] + r_w, writes=[rpp])
                blk = q0 // 128 + s
                if which == 2:
                    P.op("act", lambda e, pp=pp, blk=blk: e.copy(out=Vb[:, blk, :], in_=pp), reads=[rpp], writes=[r_Vb[blk]])
                    continue
                qb = (which * NS + s) % 2
                gain = (gq_b if which == 0 else gk_b)[:].unsqueeze(1).to_broadcast([128, 8, 64])
                head_norm(P, None, pp, [rpp, r_c], gain, qn[qb][:], [r_qn[qb]], tmp=hn_tmp)
                ptb = TN.ptr[qb]
                rptb = TN.r_ptr[qb]

                def trq(e, qb=qb, ptb=ptb):
                    ins = None
                    for hp in range(4):
                        ins = e.transpose(out=ptb[:, hp * 128:(hp + 1) * 128], in_=qn[qb][:, hp * 128:(hp + 1) * 128], identity=C.ident_bf[:])
                    return ins
                P.op("pe", trq, reads=[r_qn[qb], C.r], writes=[rptb])
                srcv = ptb[:, 0:512].rearrange("p (h t) -> p h t", h=4)
                if which == 0:
                    P.op("dve", lambda e, srcv=srcv, s=s: e.tensor_copy(out=qT[:, :, s * 128:(s + 1) * 128], in_=srcv), reads=[rptb], writes=[r_qT])
                else:
                    P.op("dve", lambda e, srcv=srcv, blk=blk: e.tensor_copy(out=KT[:, :, blk * 128:(blk + 1) * 128], in_=srcv), reads=[rptb], writes=[r_KT[blk]])
        nkb = 2 * g + 2
        units = [(hp, kb) for hp in range(4) for kb in range(nkb - 1, -1, -1)]
        NU = len(units)

        def stage1a(u):
            hp, kb = units[u]
            zb = u % 2
            P.op("pe", lambda e: e.matmul(pproj[zb][:, 0:T], lhsT=KT[0:64, hp, kb * 128:(kb + 1) * 128], rhs=qT[0:64, hp, :], start=True, stop=True),
                 reads=[r_KT[kb], r_qT], writes=[r_pproj[zb]])

        def sep(u):
            cb = u % 2
            P.op("pe", lambda e: e.matmul(pcs[cb][:, 0:8], lhsT=C.ones_bf[:], rhs=C.ones_bf[:, 0:8], start=True, stop=True), reads=[C.r], writes=[r_pc[cb]])

        def stage1b(u):
            hp, kb = units[u]
            zb = u % 2
            eb = u % NE
            lb = u % NL
            P.op("pe", lambda e: e.matmul(pproj[zb][:, T:2 * T], lhsT=KT[64:128, hp, kb * 128:(kb + 1) * 128], rhs=qT[64:128, hp, :], start=True, stop=True),
                 reads=[r_KT[kb], r_qT], writes=[r_pproj[zb]])
            P.op("act", lambda e: e.activation(out=Eb[eb][:].rearrange("p b t -> p (b t)"), in_=pproj[zb], func=AF.Exp), reads=[r_pproj[zb]], writes=[r_E[eb]])
            if kb >= 2 * g:
                base = q0 - kb * 128
                P.op("pool", lambda e: e.affine_select(out=Eb[eb][:], in_=Eb[eb][:], pattern=[[0, 2], [1, T]], compare_op=ALU.is_gt, fill=0.0, base=base, channel_multiplier=-1),
                     reads=[r_E[eb]], writes=[r_E[eb]])
            P.op("act", lambda e: e.activation(out=Lb[lb][:], in_=Eb[eb][:], func=AF.Ln, bias=1.0), reads=[r_E[eb]], writes=[r_L[lb]])

        def stage2(u):
            hp, kb = units[u]
            cb = u % 2
            eb = u % NE
            lb = u % NL
            xb = u % 2
            ab = u % NA
            lsb = hp % 2
            first = kb == nkb - 1
            cap = pcs[cb][:, :]
            L2 = Lb[lb][:].rearrange("p b t -> p (b t)")
            S2 = Ls[lsb][:].rearrange("p b t -> p (b t)")

            def cmm(e):
                ins = e.matmul(cap, lhsT=tri[:], rhs=L2, start=True, stop=first)
                if not first:
                    ins = e.matmul(cap, lhsT=C.ones_bf[:], rhs=S2, start=False, stop=True)
                return ins
            P.op("pe", cmm, reads=[r_L[lb], r_c, C.r] + ([] if first else [r_Ls[lsb]]), writes=[r_pc[cb]])
            if kb > 0:
                if first:
                    P.op("pool", lambda e: e.tensor_copy(out=Ls[lsb][:], in_=Lb[lb][:]), reads=[r_L[lb]], writes=[r_Ls[lsb]])
                else:
                    P.op("pool", lambda e: e.tensor_tensor(out=Ls[lsb][:], in0=Ls[lsb][:], in1=Lb[lb][:], op=ALU.add), reads=[r_L[lb], r_Ls[lsb]], writes=[r_Ls[lsb]])
            P.op("act", lambda e: e.activation(out=Xb[xb][:].rearrange("p b t -> p (b t)"), in_=cap, func=AF.Exp, scale=-1.0), reads=[r_pc[cb]], writes=[r_X[xb]])
            P.op("dve", lambda e: e.tensor_tensor(out=At[ab][:], in0=Eb[eb][:], in1=Xb[xb][:], op=ALU.mult), reads=[r_E[eb], r_X[xb]], writes=[r_At[ab]])

        def stage3(u):
            hp, kb = units[u]
            ab = u % NA

            def omm(e, nkb=nkb):
                ins = None
                for hh in range(2):
                    h = 2 * hp + hh
                    for s in range(NS):
                        if kb == nkb - 1 and s == 0:
                            continue
                        st = ((kb == nkb - 1) if s == 1 else (kb == nkb - 2)) and h == 0
                        ins = e.matmul(po[s][:, h * 64:(h + 1) * 64], lhsT=At[ab][:, hh, s * 128:(s + 1) * 128], rhs=Vb[:, kb, h * 64:(h + 1) * 64], start=st, stop=(kb == 0),
                                       skip_group_check=True)
                return ins
            P.op("pe", omm, reads=[r_At[ab], r_Vb[kb]], writes=r_po)
            warm()

        lvl = int(os.environ.get("ATTN_LVL", "9"))
        SK = 2
        if WB > 0:
            warm(WB)
        for n in range(NU + 2 * SK):
            if n < NU:
                stage1a(n)
            if 0 <= n - SK < NU:
                stage2(n - SK)
            elif n < NU:
                sep(n)
            if n < NU:
                stage1b(n)
            if 0 <= n - 2 * SK < NU:
                stage3(n - 2 * SK)
        if lvl < 4:
            continue
        for s in range(NS):
            head_norm(P, None, po[s][:], [r_po[s], r_c], go_b[:].rearrange("p (h d) -> p h d", d=64), ob[:, s, :], [r_ob], tmp=hn_tmp)
        if lvl < 5:
            continue
        P.op("dma", lambda e, i=i: e.dma_start(out=osb_t[i], in_=ob[:]), reads=[r_ob])
    n = P.emit()
    es.close()
    return n


DEC = 0.6065306597126334


def load_cols(P, nc, SB, name, v_ap, nk, r):
    t = SB(name, [128, nk], F32)
    P.op("dma", lambda e: e.dma_start(out=t[:], in_=v_ap.rearrange("(k p) -> p k", p=128), allow_slow_non_contiguous=True), writes=[r])
    return t


def rwkv_phase(P, nc, C, tag, src, dst, osb, W, ntok=NTOK):
    T = 256
    NS = 2
    NCH = 4
    ntiles = ntok // T
    TPS = SEQ // T
    es = ExitStack()

    def SB(name, shape, dt=F32):
        return es.enter_context(nc.sbuf_tensor(tag + name, shape, dt))

    def PS(name, shape, dt=F32):
        return es.enter_context(nc.psum_tensor(tag + name, shape, dt))

    TN = TileNorm(P, nc, C, SB, PS, T, nxt=2)
    src_t = src.rearrange("(n s p) d -> n p s d", p=128, s=NS)
    dst_t = dst.rearrange("(n s p) d -> n p s d", p=128, s=NS)
    osb_t = osb.rearrange("(n s p) d -> n p s d", p=128, s=NS)
    TN.load(src_t, 0)
    gcol, r_g = load_gain_cols(P, nc, es, tag + "gcol", W["norm_mix"], 8)
    wrw, r_wrw = load_weight_cols(P, nc, SB, "wrw", W["w_in"], 1536, 3328, gcol, r_g, piece=896)
    wout, r_wout = load_weight_cols(P, nc, SB, "wout", W["w_out"], 0, 1024, None, None, piece=1024)
    r_c = P.R()
    mu = load_cols(P, nc, SB, "mu", W["rw_mu"], 14, r_c)
    omm = SB("omm", [128, 14], F32)
    P.op("dve", lambda e: e.tensor_scalar(out=omm[:], in0=mu[:], scalar1=-1.0, scalar2=1.0, op0=ALU.mult, op1=ALU.add), reads=[r_c], writes=[r_c])
    w0c = load_cols(P, nc, SB, "w0c", W["rw_w0"], 4, r_c)
    a0c = load_cols(P, nc, SB, "a0c", W["rw_a0"], 4, r_c)
    kkc = load_cols(P, nc, SB, "kkc", W["rw_k_k"], 4, r_c)
    kac = load_cols(P, nc, SB, "kac", W["rw_k_a"], 4, r_c)
    rkc = load_cols(P, nc, SB, "rkc", W["rw_r_k"], 4, r_c)
    okac = SB("okac", [128, 4], F32)
    P.op("dve", lambda e: e.tensor_scalar(out=okac[:], in0=kac[:], scalar1=-1.0, scalar2=1.0, op0=ALU.mult, op1=ALU.add), reads=[r_c], writes=[r_c])
    stg = SB("cstg", [128, 512], F32)
    wa2 = SB("wa2", [128, 512], BF16)
    g2b = SB("g2b", [128, 512], BF16)
    P.op("dma", lambda e: e.dma_start(out=stg[0:64, :], in_=W["rw_w2"][:, :]), writes=[r_c])
    P.op("dma", lambda e: e.dma_start(out=stg[64:128, :], in_=W["rw_a2"][:, :]), writes=[r_c])
    P.op("dve", lambda e: e.tensor_copy(out=wa2[:], in_=stg[:]), reads=[r_c], writes=[r_c])
    P.op("dma", lambda e: e.dma_start(out=stg[:], in_=W["rw_g2"][:, :]), reads=[r_c], writes=[r_c])
    P.op("dve", lambda e: e.tensor_copy(out=g2b[:], in_=stg[:]), reads=[r_c], writes=[r_c])
    lnw_b = SB("lnw_b", [64, 512], F32)
    lnb_b = SB("lnb_b", [64, 512], F32)
    P.op("dma", lambda e: e.dma_start(out=lnw_b[:], in_=W["rw_ln_w"].partition_broadcast(64)), writes=[r_c])
    P.op("dma", lambda e: e.dma_start(out=lnb_b[:], in_=W["rw_ln_b"].partition_broadcast(64)), writes=[r_c])
    RKbd = SB("RKbd", [128, 4, 2], BF16)
    P.op("pool", lambda e: e.memset(RKbd[:], 0.0), writes=[r_c])
    P.op("pool", lambda e: e.tensor_copy(out=RKbd[0:64, :, 0], in_=rkc[0:64, :]), reads=[r_c], writes=[r_c])
    P.op("pool", lambda e: e.tensor_copy(out=RKbd[64:128, :, 1], in_=rkc[64:128, :]), reads=[r_c], writes=[r_c])
    BD = SB("BD", [128, 128], BF16)
    P.op("pool", lambda e: e.memset(BD[:], 0.0), writes=[r_c])
    P.op("pool", lambda e: e.memset(BD[0:64, 0:64], 1.0), writes=[r_c])
    P.op("pool", lambda e: e.memset(BD[64:128, 64:128], 1.0), writes=[r_c])
    I2 = SB("I2", [128, 64], BF16)
    P.op("pool", lambda e: e.tensor_tensor(out=I2[:], in0=C.ident_bf[:, 0:64], in1=C.ident_bf[:, 64:128], op=ALU.add), reads=[C.r], writes=[r_c])
    mSL = SB("mSL", [64, 64], BF16)
    mSU = SB("mSU", [64, 64], BF16)
    mIU = SB("mIU", [64, 64], BF16)
    mC = SB("mC", [128, 128], BF16)
    ob64 = C.ones_bf[0:64, 0:64]
    P.op("pool", lambda e: e.affine_select(out=mSL[:], in_=ob64, pattern=[[-1, 64]], compare_op=ALU.is_gt, fill=0.0, base=0, channel_multiplier=1), reads=[C.r], writes=[r_c])
    P.op("pool", lambda e: e.affine_select(out=mSU[:], in_=ob64, pattern=[[1, 64]], compare_op=ALU.is_gt, fill=0.0, base=0, channel_multiplier=-1), reads=[C.r], writes=[r_c])
    P.op("pool", lambda e: e.affine_select(out=mIU[:], in_=ob64, pattern=[[1, 64]], compare_op=ALU.is_ge, fill=0.0, base=0, channel_multiplier=-1), reads=[C.r], writes=[r_c])
    mA = SB("mA", [64, 192], BF16)
    mB = SB("mB", [64, 192], BF16)
    P.op("pool", lambda e: e.memset(mA[:], 1.0), writes=[r_c])
    P.op("pool", lambda e: e.memset(mB[:], 1.0), writes=[r_c])
    P.op("pool", lambda e: e.tensor_copy(out=mA[:, 0:64], in_=mSL[:]), reads=[r_c], writes=[r_c])
    P.op("pool", lambda e: e.tensor_copy(out=mA[:, 128:192], in_=mSL[:]), reads=[r_c], writes=[r_c])
    P.op("pool", lambda e: e.tensor_copy(out=mB[:, 0:64], in_=mIU[:]), reads=[r_c], writes=[r_c])
    P.op("pool", lambda e: e.tensor_copy(out=mB[:, 128:192], in_=mSU[:]), reads=[r_c], writes=[r_c])
    P.op("pool", lambda e: e.memset(mC[:], 1.0), writes=[r_c])
    P.op("pool", lambda e: e.tensor_copy(out=mC[0:64, 0:64], in_=mIU[:]), reads=[r_c], writes=[r_c])
    cm = SB("cm", [128, T], F32)
    P.op("pool", lambda e: e.memset(cm[:], 1.0), writes=[r_c])
    P.op("pool", lambda e: e.memset(cm[:].rearrange("p (c t) -> p c t", t=64)[:, :, 0:1], 0.0), writes=[r_c])

    U = SB("U", [128, 14, T + 2], F32)
    r_U = P.R()
    X = SB("X", [128, 14, T], F32)
    r_X = [P.R() for _ in range(14)]
    tsh = SB("tsh", [128, T], F32)
    r_tsh = P.R()
    twxa = SB("twxa", [128, T], BF16)
    r_twxa = P.R()
    sgg = SB("sgg", [128, T], BF16)
    r_sgg = P.R()
    NB = 6
    NSET = 2
    bbs = [[SB("b%d_%d" % (j, i), [128, T], F32) for i in range(NB)] for j in range(NSET)]
    r_bbs = [[P.R() for _ in range(NB)] for j in range(NSET)]
    sqbs = [SB("sqb%d" % j, [128, T], BF16) for j in range(NSET)]
    r_sqbs = [P.R() for j in range(NSET)]
    rkbs = [SB("rkb%d" % j, [128, T], BF16) for j in range(NSET)]
    r_rkbs = [P.R() for j in range(NSET)]
    ARD = [SB("ARD%d" % i, [128, NCH, 192], BF16) for i in range(4)]
    r_ARD = [P.R() for _ in range(4)]
    BKI = [SB("BKI%d" % i, [128, NCH, 192], BF16) for i in range(4)]
    r_BKI = [P.R() for _ in range(4)]
    for hp in range(4):
        P.op("pool", lambda e, hp=hp: e.tensor_copy(out=BKI[hp][:, :, 64:128], in_=I2[:].unsqueeze(1).to_broadcast([128, NCH, 64])), reads=[r_c], writes=[r_BKI[hp]])
    NSV = 6
    SV = [SB("SV%d" % i, [128, 8, 64], F32) for i in range(NSV)]
    r_SVv = [P.R() for _ in range(NSV)]
    r_SVs = [P.R() for _ in range(NSV)]
    Gt = SB("Gt", [64, NCH, 512], F32)
    r_Gt = [P.R() for _ in range(NCH)]
    bsc = SB("bsc", [64, NCH, 8], F32)
    r_bsc = P.R()
    ZPP = [[SB("ZPP%d_%d" % (g_, j), [64, 4, 256], BF16) for j in range(2)] for g_ in range(2)]
    r_Zb = [[P.R() for j in range(2)] for g_ in range(2)]
    r_PP = [[P.R() for j in range(2)] for g_ in range(2)]
    BRB = [SB("BRB%d" % g_, [64, 4, 192], BF16) for g_ in range(2)]
    r_BRB = [P.R() for _ in range(2)]
    LB = [SB("LB%d" % g_, [128, 4, 128], F32) for g_ in range(2)]
    r_LB = [P.R() for _ in range(2)]
    LBa = [SB("LBa%d" % g_, [128, 4, 128], F32) for g_ in range(2)]
    r_LBa = [P.R() for _ in range(2)]
    Yc = SB("Yc", [64, 512], F32)
    r_Yc = P.R()
    t1 = SB("t1", [64, 512], F32)
    r_t1 = P.R()
    t2 = SB("t2", [64, 512], F32)
    r_t2 = P.R()
    st = SB("st", [64, 48], F32)
    r_st = P.R()
    orw = SB("orw", [64, NCH, 512], BF16)
    r_orw = P.R()
    osbt = SB("osbt", [128, NS, 512], BF16)
    r_osbt = P.R()
    oT = SB("oT", [128, 8, T], BF16)
    r_oTa = P.R()
    r_oTb = P.R()

    pA = [PS("pA%d" % i, [128, 512], F32) for i in range(2)]
    r_pA = [P.RP() for _ in range(2)]
    pr = [PS("pr%d" % i, [128, 1024], F32) for i in range(2)]
    r_pr = [P.RP() for _ in range(2)]
    pai = [0]

    def nextA():
        b = pai[0] % 2
        pai[0] += 1
        return pA[b], r_pA[b]

    ev2 = rr(["act", "dve"])

    for i in range(ntiles):
        g = i % TPS
        x, rx = TN.norm(i)
        if i + 1 < ntiles:
            TN.load(src_t, i + 1)
        hT, r_hT = TN.hT, TN.r_hT
        rwl = float(os.environ.get("RW_LVL", "9"))
        if rwl < 0.5:
            continue
        P.op("dma", lambda e, i=i: e.dma_start(out=osbt[:], in_=osb_t[i]), writes=[r_osbt])
        if g == 0:
            P.op("pool", lambda e: e.memset(U[:, :, 1:2], 0.0), writes=[r_U])
        if rwl < 0.75:
            continue
        for f in range(14):
            pu, rpu = nextA()

            def proj(e, f=f, pu=pu):
                ins = None
                for k in range(8):
                    ins = e.matmul(pu[:, 0:T], lhsT=wrw[:, k, f * 128:(f + 1) * 128], rhs=hT[:, k, :], start=(k == 0), stop=(k == 7))
                return ins
            P.op("pe", proj, reads=[r_hT] + r_wrw, writes=[rpu])
            P.op("act", lambda e, f=f, pu=pu: e.activation(out=X[:, f, :], in_=pu[:, 0:T], func=AF.Copy, scale=omm[:, f:f + 1]), reads=[rpu, r_c], writes=[r_X[f]])
            if rwl < 0.85:
                continue
            P.op("dve", lambda e, f=f, pu=pu: e.tensor_copy(out=U[:, f, 2:T + 2], in_=pu[:, 0:T]), reads=[rpu], writes=[r_U])
            if rwl < 0.9:
                continue
            P.op("dve", lambda e, f=f: e.scalar_tensor_tensor(out=X[:, f, :], in0=U[:, f, 1:T + 1], scalar=mu[:, f:f + 1], in1=X[:, f, :], op0=ALU.mult, op1=ALU.add),
                 reads=[r_U, r_X[f], r_c], writes=[r_X[f]])
        if rwl >= 0.95:
            P.op("pool", lambda e: e.tensor_copy(out=U[:, :, 1:2], in_=U[:, :, T + 1:T + 2]), reads=[r_U], writes=[r_U])
        if rwl < 2:
            continue
        P.op("act", lambda e: e.activation(out=twxa[0:64, :], in_=X[0:64, 12, :], func=AF.Tanh), reads=[r_X[12]], writes=[r_twxa])
        P.op("dve", lambda e: e.tensor_copy(out=twxa[64:128, :], in_=X[64:128, 12, :]), reads=[r_X[12]], writes=[r_twxa])
        P.op("act", lambda e: e.activation(out=sgg[:], in_=X[:, 13, :], func=AF.Sigmoid), reads=[r_X[13]], writes=[r_sgg])
        for c in range(NCH):
            pg, rpg = nextA()
            P.op("pe", lambda e, c=c, pg=pg: e.matmul(pg[0:64, :], lhsT=sgg[:, c * 64:(c + 1) * 64], rhs=g2b[:], start=True, stop=True), reads=[r_sgg, r_c], writes=[rpg])
            P.op("act", lambda e, c=c, pg=pg: e.copy(out=Gt[:, c, :], in_=pg[0:64, :]), reads=[rpg], writes=[r_Gt[c]])
        if rwl < 3:
            continue
        def prep_hp(hp):
            xr, xk, xv = X[:, hp, :], X[:, 4 + hp, :], X[:, 8 + hp, :]
            rxr, rxk, rxv = r_X[hp], r_X[4 + hp], r_X[8 + hp]
            b0, b1, b2, b3, b4, b5 = bbs[hp % NSET]
            q0_, q1_, q2_, q3_, q4_, q5_ = r_bbs[hp % NSET]
            sqb, r_sqb, rkb, r_rkb = sqbs[hp % NSET], r_sqbs[hp % NSET], rkbs[hp % NSET], r_rkbs[hp % NSET]
            pw, rpw = nextA()
            yield
            P.op("pe", lambda e, hp=hp, pw=pw: e.matmul(pw[:, 0:T], lhsT=wa2[0:64, hp * 128:(hp + 1) * 128], rhs=twxa[0:64, :], start=True, stop=True), reads=[r_twxa, r_c], writes=[rpw])
            yield
            P.op("act", lambda e, hp=hp, pw=pw: e.activation(out=b0[:], in_=pw[:, 0:T], func=AF.Sigmoid, bias=w0c[:, hp:hp + 1]), reads=[rpw, r_c], writes=[q0_])
            pa_, rpa = nextA()
            yield
            P.op("pe", lambda e, hp=hp, pa_=pa_: e.matmul(pa_[:, 0:T], lhsT=wa2[64:128, hp * 128:(hp + 1) * 128], rhs=twxa[64:128, :], start=True, stop=True), reads=[r_twxa, r_c], writes=[rpa])
            yield
            P.op("act", lambda e, hp=hp, pa_=pa_: e.activation(out=b3[:], in_=pa_[:, 0:T], func=AF.Sigmoid, bias=a0c[:, hp:hp + 1]), reads=[rpa, r_c], writes=[q3_])
            yield
            P.op("dve", lambda e: e.tensor_tensor_scan(out=b1[:], data0=cm[:], data1=b0[:], initial=0.0, op0=ALU.mult, op1=ALU.add), reads=[q0_, r_c], writes=[q1_])
            yield
            P.op("pool", lambda e: e.tensor_tensor(out=b2[:], in0=b1[:], in1=b0[:], op=ALU.subtract), reads=[q0_, q1_], writes=[q2_])
            yield
            P.op("act", lambda e: e.activation(out=b0[:], in_=b1[:], func=AF.Exp, scale=-DEC), reads=[q1_, q2_], writes=[q0_])
            yield
            P.op("act", lambda e: e.activation(out=b1[:], in_=b1[:], func=AF.Exp, scale=DEC), reads=[q1_, q0_], writes=[q1_])
            yield
            P.op("act", lambda e: e.activation(out=b2[:], in_=b2[:], func=AF.Exp, scale=-DEC), reads=[q2_], writes=[q2_])
            yield
            P.op("act", lambda e, hp=hp, xk=xk: e.activation(out=b4[:], in_=xk, func=AF.Copy, scale=kkc[:, hp:hp + 1]), reads=[rxk, r_c], writes=[q4_])
            yield
            P.op("pool", lambda e: e.tensor_tensor(out=sqb[:], in0=b4[:], in1=b4[:], op=ALU.mult), reads=[q4_], writes=[r_sqb])
            pss, rpss = nextA()
            yield
            P.op("pe", lambda e, pss=pss: e.matmul(pss[:, 0:T], lhsT=BD[:], rhs=sqb[:], start=True, stop=True), reads=[r_sqb, r_c], writes=[rpss])
            yield
            P.op("act", lambda e, pss=pss: e.activation(out=b5[:], in_=pss[:, 0:T], func=AF.Sqrt), reads=[rpss], writes=[q5_])
            yield
            P.op("dve", lambda e: e.tensor_scalar(out=b5[:], in0=b5[:], scalar1=1e-12, scalar2=None, op0=ALU.max), reads=[q5_], writes=[q5_])
            yield
            P.op("dve", lambda e: e.reciprocal(out=b5[:], in_=b5[:]), reads=[q5_], writes=[q5_])
            yield
            P.op("dve", lambda e: e.tensor_tensor(out=b4[:], in0=b4[:], in1=b5[:], op=ALU.mult), reads=[q4_, q5_], writes=[q4_])
            yield
            P.op("dve", lambda e, hp=hp: e.tensor_scalar(out=b5[:], in0=b3[:], scalar1=kac[:, hp:hp + 1], scalar2=okac[:, hp:hp + 1], op0=ALU.mult, op1=ALU.add),
                 reads=[q3_, q4_, r_c], writes=[q5_])
            yield
            P.op("pool", lambda e, xk=xk: e.tensor_tensor(out=b5[:], in0=xk, in1=b5[:], op=ALU.mult), reads=[rxk, q5_], writes=[q5_])

            def v3(ap):
                return ap.rearrange("p (c t) -> p c t", t=64)
            A_, B_ = ARD[hp], BKI[hp]
            yield
            P.op("dve", lambda e, A_=A_: e.scalar_tensor_tensor(out=A_[:, :, 128:192], in0=v3(b4[:]), scalar=-1.0, in1=v3(b2[:]), op0=ALU.mult, op1=ALU.mult),
                 reads=[q4_, q2_], writes=[r_ARD[hp]])
            yield
            P.op("pool", lambda e, A_=A_, xr=xr: e.tensor_tensor(out=A_[:, :, 0:64], in0=v3(xr), in1=v3(b0[:]), op=ALU.mult), reads=[rxr, q0_], writes=[r_ARD[hp]])
            yield
            P.op("dve", lambda e, A_=A_: e.tensor_tensor(out=A_[:, :, 64:128], in0=I2[:].unsqueeze(1).to_broadcast([128, NCH, 64]),
                                                         in1=v3(b0[:])[:, :, 63:64].to_broadcast([128, NCH, 64]), op=ALU.mult), reads=[q0_, r_c], writes=[r_ARD[hp]])
            yield
            P.op("pool", lambda e: e.tensor_tensor(out=b2[:], in0=b4[:], in1=b3[:], op=ALU.mult), reads=[q4_, q3_, q2_], writes=[q2_])
            yield
            P.op("dve", lambda e, B_=B_: e.tensor_tensor(out=B_[:, :, 128:192], in0=v3(b2[:]), in1=v3(b1[:]), op=ALU.mult), reads=[q2_, q1_], writes=[r_BKI[hp]])
            yield
            P.op("pool", lambda e, B_=B_: e.tensor_tensor(out=B_[:, :, 0:64], in0=v3(b5[:]), in1=v3(b1[:]), op=ALU.mult), reads=[q5_, q1_], writes=[r_BKI[hp]])
            yield
            P.op("dve", lambda e, xr=xr: e.tensor_tensor(out=rkb[:], in0=xr, in1=b5[:], op=ALU.mult), reads=[rxr, q5_], writes=[r_rkb])
            pbn, rpbn = pr[1], r_pr[1]

            def bon(e, hp=hp, pbn=pbn):
                ins = None
                for c in range(NCH):
                    ins = e.matmul(pbn[0:64, c * 8 + hp * 2: c * 8 + hp * 2 + 2], lhsT=rkb[:, c * 64:(c + 1) * 64], rhs=RKbd[:, hp, :], start=True, stop=True)
                return ins
            yield
            P.op("pe", bon, reads=[r_rkb, r_c], writes=[rpbn])
            if hp == 3:
                P.op("act", lambda e, pbn=pbn: e.copy(out=bsc[:].rearrange("p c h -> p (c h)"), in_=pbn[0:64, 0:NCH * 8]), reads=[rpbn], writes=[r_bsc])
        for pair in ((0, 1), (2, 3)):
            gens = [prep_hp(h_) for h_ in pair]
            while gens:
                for g_ in list(gens):
                    try:
                        next(g_)
                    except StopIteration:
                        gens.remove(g_)
        if rwl < 4:
            continue
        for c in range(NCH):
            slot = (i * NCH + c) % NSV
            pv, rpv = nextA()

            def vt(e, c=c, pv=pv):
                ins = None
                for hp in range(4):
                    ins = e.transpose(out=pv[0:64, hp * 128:(hp + 1) * 128], in_=X[:, 8 + hp, c * 64:(c + 1) * 64], identity=C.ident_f[:])
                return ins
            P.op("pe", vt, reads=[r_X[8], r_X[9], r_X[10], r_X[11], C.r], writes=[rpv])
            P.op("act", lambda e, slot=slot, pv=pv: e.copy(out=SV[slot][0:64, :, :].rearrange("p h d -> p (h d)"), in_=pv[0:64, :]), reads=[rpv], writes=[r_SVv[slot]])
        if g == 0:
            slot0 = (i * NCH) % NSV
            P.op("pool", lambda e, slot0=slot0: e.memset(SV[slot0][64:128, :, :], 0.0), writes=[r_SVs[slot0]])
        if rwl < 4.05:
            continue
        for c in range(NCH):
            slot = (i * NCH + c) % NSV
            nslot = (i * NCH + c + 1) % NSV
            hgs = (0, 1)

            def hops(hg, h4):
                return h4, hg * 64

            for hg in hgs:
                def mma(e, hg=hg, c=c):
                    ins = None
                    for h4 in range(4):
                        hp, pb = hops(hg, h4)
                        ins = e.matmul(pr[hg][0:64, h4 * 256:h4 * 256 + 192], lhsT=ARD[hp][pb:pb + 64, c, 128:192], rhs=BKI[hp][pb:pb + 64, c, 0:192], start=True, stop=True)
                    return ins
                P.op("pe", mma, reads=r_ARD + r_BKI, writes=[r_pr[hg]])
                pv4 = pr[hg][0:64, :].rearrange("p (h w) -> p h w", w=256)
                P.op("dve", lambda e, hg=hg, pv4=pv4: e.tensor_tensor(out=ZPP[hg][0][:, :, 0:192], in0=pv4[:, :, 0:192], in1=mA[:].unsqueeze(1).to_broadcast([64, 4, 192]), op=ALU.mult),
                     reads=[r_pr[hg], r_c], writes=[r_Zb[hg][0], r_PP[hg][0]])
            for hg in hgs:
                def mmb(e, hg=hg, c=c):
                    ins = None
                    for h4 in range(4):
                        hp, pb = hops(hg, h4)
                        ins = e.matmul(pr[hg][0:64, h4 * 256:h4 * 256 + 192], lhsT=BKI[hp][pb:pb + 64, c, 128:192], rhs=ARD[hp][pb:pb + 64, c, 0:192], start=True, stop=True)
                    return ins
                P.op("pe", mmb, reads=r_ARD + r_BKI, writes=[r_pr[hg]])
                pv4 = pr[hg][0:64, :].rearrange("p (h w) -> p h w", w=256)
                P.op("dve", lambda e, hg=hg, pv4=pv4: e.tensor_tensor(out=BRB[hg][:], in0=pv4[:, :, 0:192], in1=mB[:].unsqueeze(1).to_broadcast([64, 4, 192]), op=ALU.mult),
                     reads=[r_pr[hg], r_c], writes=[r_BRB[hg]])
            for hg in hgs:
                def mmc0(e, hg=hg, c=c):
                    ins = None
                    for h4 in range(4):
                        hp, pb = hops(hg, h4)
                        ins = e.matmul(pr[hg][:, h4 * 256:h4 * 256 + 128], lhsT=BKI[hp][pb:pb + 64, c, 0:128], rhs=ARD[hp][pb:pb + 64, c, 0:128], start=True, stop=True)
                    return ins
                P.op("pe", mmc0, reads=r_ARD + r_BKI, writes=[r_pr[hg]])
                pv4f = pr[hg][:, :].rearrange("p (h w) -> p h w", w=256)
                P.op("dve", lambda e, hg=hg, pv4f=pv4f: e.tensor_tensor(out=LBa[hg][:], in0=pv4f[:, :, 0:128], in1=mC[:].unsqueeze(1).to_broadcast([128, 4, 128]), op=ALU.mult),
                     reads=[r_pr[hg], r_c], writes=[r_LBa[hg]])
            for lvl in range(6):
                cur, nxt = lvl % 2, (lvl + 1) % 2
                for hg in hgs:
                    def lv(e, hg=hg, cur=cur, lvl=lvl):
                        ins = None
                        for h4 in range(4):
                            zp = ZPP[hg][cur]
                            Pm = BRB[hg][:, h4, 128:192] if lvl == 0 else zp[:, h4, 192:256]
                            w = 192 if lvl < 5 else 128
                            ins = e.matmul(pr[hg][0:64, h4 * 256:h4 * 256 + w], lhsT=Pm, rhs=zp[:, h4, 0:w], start=True, stop=True)
                            if lvl < 5:
                                ins = e.matmul(pr[hg][0:64, h4 * 256 + 192:h4 * 256 + 256], lhsT=zp[:, h4, 128:192], rhs=Pm, start=True, stop=True)
                        return ins
                    P.op("pe", lv, reads=[r_Zb[hg][cur], r_PP[hg][cur]] + ([r_BRB[hg]] if lvl == 0 else []), writes=[r_pr[hg]])
                    pv4 = pr[hg][0:64, :].rearrange("p (h w) -> p h w", w=256)
                    if lvl < 5:
                        P.op("act", lambda e, hg=hg, nxt=nxt, pv4=pv4: e.copy(out=ZPP[hg][nxt][:, :, 128:256], in_=pv4[:, :, 128:256]), reads=[r_pr[hg]], writes=[r_PP[hg][nxt]])
                    P.op("dve", lambda e, hg=hg, nxt=nxt, cur=cur, pv4=pv4: e.tensor_tensor(out=ZPP[hg][nxt][:, :, 0:128], in0=pv4[:, :, 0:128], in1=ZPP[hg][cur][:, :, 0:128], op=ALU.add),
                         reads=[r_pr[hg], r_Zb[hg][cur]], writes=[r_Zb[hg][nxt]])
            for hg in hgs:
                def mmc(e, hg=hg, c=c):
                    ins = None
                    for h4 in range(4):
                        o = pr[hg][:, h4 * 256:h4 * 256 + 128]
                        ins = e.matmul(o, lhsT=ZPP[hg][0][:, h4, 0:128], rhs=BRB[hg][:, h4, 0:128], start=True, stop=True)
                    return ins
                P.op("pe", mmc, reads=[r_Zb[hg][0], r_BRB[hg]], writes=[r_pr[hg]])
                pv4f = pr[hg][:, :].rearrange("p (h w) -> p h w", w=256)
                P.op("dve", lambda e, hg=hg, pv4f=pv4f: e.tensor_tensor(out=LB[hg][:], in0=pv4f[:, :, 0:128], in1=LBa[hg][:], op=ALU.add),
                     reads=[r_pr[hg], r_LBa[hg]], writes=[r_LB[hg]])
            if rwl < 4.5:
                continue
            pS, rpS = nextA()

            def seq(e, slot=slot, pS=pS):
                ins = None
                for h in range(8):
                    ins = e.matmul(pS[:, h * 64:(h + 1) * 64], lhsT=LB[h % 2][:, h // 2, :], rhs=SV[slot][:, h, :], start=True, stop=True)
                return ins
            P.op("pe", seq, reads=r_LB + [r_SVv[slot], r_SVs[slot]], writes=[rpS])
            P.op("act", lambda e, nslot=nslot, pS=pS: e.copy(out=SV[nslot][64:128, :, :].rearrange("p h d -> p (h d)"), in_=pS[64:128, :]), reads=[rpS], writes=[r_SVs[nslot]])
            P.op("dve", lambda e, pS=pS: e.tensor_copy(out=Yc[:], in_=pS[0:64, :]), reads=[rpS], writes=[r_Yc])
            if rwl < 6:
                continue
            Y3 = Yc[:].rearrange("p (h d) -> p h d", d=64)
            T13 = t1[:].rearrange("p (h d) -> p h d", d=64)
            T23 = t2[:].rearrange("p (h d) -> p h d", d=64)
            V3 = SV[slot][0:64, :, :]
            P.op("act", lambda e: e.activation(out=t1[:], in_=Yc[:], func=AF.Square), reads=[r_Yc], writes=[r_t1])
            P.op("dve", lambda e, Y3=Y3: e.tensor_reduce(out=st[:, 0:8], in_=Y3, axis=AX.X, op=ALU.add), reads=[r_Yc], writes=[r_st])
            P.op("dve", lambda e, T13=T13: e.tensor_reduce(out=st[:, 8:16], in_=T13, axis=AX.X, op=ALU.add), reads=[r_t1], writes=[r_st])
            P.op("dve", lambda e: e.tensor_scalar(out=st[:, 16:24], in0=st[:, 0:8], scalar1=1.0 / 64, scalar2=None, op0=ALU.mult), reads=[r_st], writes=[r_st])
            P.op("dve", lambda e: e.tensor_tensor(out=st[:, 24:32], in0=st[:, 16:24], in1=st[:, 16:24], op=ALU.mult), reads=[r_st], writes=[r_st])
            P.op("dve", lambda e: e.scalar_tensor_tensor(out=st[:, 32:40], in0=st[:, 8:16], scalar=1.0 / 64, in1=st[:, 24:32], op0=ALU.mult, op1=ALU.subtract),
                 reads=[r_st], writes=[r_st])
            P.op("act", lambda e: e.activation(out=st[:, 40:48], in_=st[:, 32:40], func=AF.Sqrt, bias=LNX_EPS), reads=[r_st], writes=[r_st])
            P.op("dve", lambda e: e.reciprocal(out=st[:, 40:48], in_=st[:, 40:48]), reads=[r_st], writes=[r_st])
            P.op("pool", lambda e, Y3=Y3, T13=T13: e.tensor_tensor(out=T13, in0=Y3, in1=st[:, 16:24].unsqueeze(2).to_broadcast([64, 8, 64]), op=ALU.subtract),
                 reads=[r_Yc, r_st, r_t1], writes=[r_t1])
            P.op("pool", lambda e, T13=T13: e.tensor_tensor(out=T13, in0=T13, in1=st[:, 40:48].unsqueeze(2).to_broadcast([64, 8, 64]), op=ALU.mult),
                 reads=[r_st, r_t1], writes=[r_t1])
            P.op("pool", lambda e: e.tensor_tensor(out=t1[:], in0=t1[:], in1=lnw_b[:], op=ALU.mult), reads=[r_t1, r_c], writes=[r_t1])
            P.op("pool", lambda e: e.tensor_tensor(out=t1[:], in0=t1[:], in1=lnb_b[:], op=ALU.add), reads=[r_t1, r_c], writes=[r_t1])
            P.op("pool", lambda e, c=c, V3=V3, T23=T23: e.tensor_tensor(out=T23, in0=V3, in1=bsc[:, c, :].unsqueeze(2).to_broadcast([64, 8, 64]), op=ALU.mult),
                 reads=[r_SVv[slot], r_bsc], writes=[r_t2])
            P.op("pool", lambda e: e.tensor_tensor(out=t1[:], in0=t1[:], in1=t2[:], op=ALU.add), reads=[r_t1, r_t2], writes=[r_t1])
            P.op("pool", lambda e, c=c: e.tensor_tensor(out=orw[:, c, :], in0=t1[:], in1=Gt[:, c, :], op=ALU.mult), reads=[r_t1, r_Gt[c]], writes=[r_orw])
        if rwl < 7:
            continue
        ptb, rptb = TN.ptr[0], TN.r_ptr[0]

        def tro(e, ptb=ptb):
            ins = None
            for s in range(NS):
                for hp in range(4):
                    ins = e.transpose(out=ptb[:, hp * T + s * 128: hp * T + (s + 1) * 128], in_=osbt[:, s, hp * 128:(hp + 1) * 128], identity=C.ident_bf[:])
            return ins
        P.op("pe", tro, reads=[r_osbt, C.r], writes=[rptb])
        P.op("act", lambda e, ptb=ptb: e.copy(out=oT[:, 0:4, :], in_=ptb[:, 0:4 * T].rearrange("p (h t) -> p h t", h=4)), reads=[rptb], writes=[r_oTa])
        ptc, rptc = TN.ptr[1], TN.r_ptr[1]

        def trr(e, ptc=ptc):
            ins = None
            for c in range(NCH):
                for hp in range(4):
                    ins = e.transpose(out=ptc[:, hp * T + c * 64: hp * T + (c + 1) * 64], in_=orw[:, c, hp * 128:(hp + 1) * 128], identity=C.ident_bf[0:64, 0:64])
            return ins
        P.op("pe", trr, reads=[r_orw, C.r], writes=[rptc])
        P.op("dve", lambda e, ptc=ptc: e.tensor_copy(out=oT[:, 4:8, :], in_=ptc[:, 0:4 * T].rearrange("p (h t) -> p h t", h=4)), reads=[rptc], writes=[r_oTb])
        for s in range(NS):
            for hf in range(2):
                po, rpo = nextA()

                def wo(e, s=s, hf=hf, po=po):
                    ins = None
                    for f in range(8):
                        ins = e.matmul(po[:], lhsT=oT[:, f, s * 128:(s + 1) * 128], rhs=wout[:, f, hf * 512:(hf + 1) * 512], start=(f == 0), stop=(f == 7))
                    return ins
                P.op("pe", wo, reads=[r_oTa, r_oTb] + r_wout, writes=[rpo])
                P.op("dve", lambda e, s=s, hf=hf, po=po, x=x: e.tensor_tensor(out=x[:, s, hf * 512:(hf + 1) * 512], in0=po[:], in1=x[:, s, hf * 512:(hf + 1) * 512], op=ALU.add),
                     reads=[rpo, rx], writes=[rx])
        P.op("dma", lambda e, i=i, x=x: e.dma_start(out=dst_t[i], in_=x[:]), reads=[rx])
        if DEBUG and i == 0:
            for nm, t, rs, shp, dt in [("X", X, r_X, [128, 14 * T], F32), ("orw", orw, [r_orw], [64, NCH * 512], BF16), ("Gt", Gt, r_Gt, [64, NCH * 512], F32),
                                       ("bsc", bsc, [r_bsc], [64, NCH * 8], F32)]:
                o = nc.dram_tensor("dbg_" + nm, shp, F32, kind="ExternalOutput").ap()
                fl = t[:].rearrange("p a b -> p (a b)")
                if dt == BF16:
                    tmpf = SB("dbgf_" + nm, shp, F32)
                    P.op("dve", lambda e, tmpf=tmpf, fl=fl: e.tensor_copy(out=tmpf[:], in_=fl), reads=rs, writes=[r_c])
                    P.op("dma", lambda e, o=o, tmpf=tmpf: e.dma_start(out=o[:, :], in_=tmpf[:]), reads=[r_c])
                else:
                    P.op("dma", lambda e, o=o, fl=fl: e.dma_start(out=o[:, :], in_=fl), reads=rs)
    n = P.emit()
    es.close()
    return n


W_NAMES = ["norm_ffn1", "ffn1_gate", "ffn1_up", "ffn1_down", "norm_mix", "w_in", "sb_q_norm", "sb_k_norm", "sb_out_norm",
           "rw_mu", "rw_w0", "rw_w2", "rw_a0", "rw_a2", "rw_g2", "rw_k_k", "rw_k_a", "rw_r_k", "rw_ln_w", "rw_ln_b",
           "w_out", "norm_ffn2", "ffn2_gate", "ffn2_up", "ffn2_down"]
W_SHAPES = {"norm_ffn1": [D], "ffn1_gate": [D, DFF], "ffn1_up": [D, DFF], "ffn1_down": [DFF, D], "norm_mix": [D], "w_in": [D, PROJ],
            "sb_q_norm": [64], "sb_k_norm": [64], "sb_out_norm": [512], "rw_mu": [1792], "rw_w0": [512], "rw_w2": [64, 512],
            "rw_a0": [512], "rw_a2": [64, 512], "rw_g2": [128, 512], "rw_k_k": [512], "rw_k_a": [512], "rw_r_k": [512],
            "rw_ln_w": [512], "rw_ln_b": [512], "w_out": [D, D], "norm_ffn2": [D], "ffn2_gate": [D, DFF], "ffn2_up": [D, DFF],
            "ffn2_down": [DFF, D]}


def build(phases=("ffn1", "attn", "rwkv", "ffn2"), ntok=NTOK):
    nc = bass.Bass("TRN2", target_bir_lowering=False)
    x = nc.dram_tensor("x", [ntok, D], F32, kind="ExternalInput").ap()
    W = {n: nc.dram_tensor(n, W_SHAPES[n], F32, kind="ExternalInput").ap() for n in W_NAMES}
    out = nc.dram_tensor("out", [ntok, D], F32, kind="ExternalOutput").ap()
    s1 = nc.dram_tensor("scr1", [ntok, D], F32, kind="Internal").ap()
    s2 = nc.dram_tensor("scr2", [ntok, D], F32, kind="Internal").ap()
    osb = nc.dram_tensor("osb", [ntok, 512], BF16, kind="ExternalOutput" if (DEBUG and not os.environ.get("OSB_INT")) else "Internal").ap()
    es = ExitStack()
    P = Prog(nc, es)
    C = make_consts(P, nc, es)
    init_consts(P, C)
    chain = list(phases)
    cur = x
    for pi, ph in enumerate(chain):
        lastp = pi == max(j for j, p_ in enumerate(chain) if p_ != "attn")
        dst = out if lastp else (s1 if cur is not s1 else s2)
        if ph == "attn":
            attn_phase(P, nc, C, "at", cur, osb, W, ntok)
            continue
        if ph == "rwkv":
            rwkv_phase(P, nc, C, "rw", cur, dst, osb, W, ntok)
            cur = dst
            continue
        if ph == "ffn1":
            ffn_phase(P, nc, C, "f1", cur, dst, W["norm_ffn1"], W["ffn1_gate"], W["ffn1_up"], W["ffn1_down"], ntok)
        elif ph == "ffn2":
            ffn_phase(P, nc, C, "f2", cur, dst, W["norm_ffn2"], W["ffn2_gate"], W["ffn2_up"], W["ffn2_down"], ntok)
        else:
            raise NotImplementedError(ph)
        cur = dst
    P.op("pool", lambda e: e.memset(C.ones_f[:, 0:1], 1.0), writes=[C.r])
    P.emit(final=True)
    es.close()
    return nc


_CACHE = {}


def kernel(**inputs):
    x = np.ascontiguousarray(np.asarray(inputs["x"], dtype=np.float32))
    B = x.shape[0]
    ncores = 8
    per = B // ncores
    key = "full"
    if key not in _CACHE:
        _CACHE[key] = build()
    nc = _CACHE[key]
    wmap = {n: np.ascontiguousarray(np.asarray(inputs[n], dtype=np.float32).reshape(W_SHAPES[n])) for n in W_NAMES}
    in_maps = []
    for c in range(ncores):
        m = {"x": x[c * per:(c + 1) * per].reshape(per * SEQ, D)}
        m.update(wmap)
        in_maps.append(m)
    res = run_bass_kernel_spmd(nc, in_maps, core_ids=list(range(ncores)))
    outs = [np.asarray(r["out"]).reshape(per, SEQ, D) for r in res.results]
    return np.concatenate(outs, axis=0).astype(np.float32)
```

```python
import os
from contextlib import ExitStack
import numpy as np
import concourse.bass as bass
import concourse.mybir as mybir
from concourse.bass_utils import run_bass_kernel_spmd

F32 = mybir.dt.float32
BF16 = mybir.dt.bfloat16
AF = mybir.ActivationFunctionType
ALU = mybir.AluOpType
AX = mybir.AxisListType

D = 1024
SEQ = 2048
NTOK = 4096
DFF = 2816
NFF = DFF // 128
PROJ = 3328
RMS_EPS = 1e-6
LNX_EPS = 64e-5

COMPUTE = ("pe", "act", "dve", "pool")
N_DMA_SEMS = 24
DEBUG = False


class Res:
    __slots__ = ("name", "w", "r", "psum")

    def __init__(self, name="", psum=False):
        self.name = name
        self.w = None
        self.r = []
        self.psum = psum


class Op:
    __slots__ = ("eng", "fn", "deps", "is_dma", "sem", "cnt", "signal", "idx", "queue")


class Prog:
    def __init__(self, nc, es):
        self.nc = nc
        self.ops = []
        self.res = []
        self.sems = {e: es.enter_context(nc.semaphore("s_" + e)) for e in COMPUTE}
        self.dsems = [es.enter_context(nc.semaphore("d%d" % i)) for i in range(N_DMA_SEMS)]
        self.cnt = {e: 0 for e in COMPUTE}
        self.dcnt = [0] * N_DMA_SEMS
        self.nd = 0
        self.base = {}

    def R(self, name="", psum=False):
        r = Res(name, psum)
        self.res.append(r)
        return r

    def RP(self):
        return self.R("", True)

    def op(self, eng, fn, reads=(), writes=(), queue=None):
        o = Op()
        o.idx = len(self.ops)
        o.eng = eng
        o.fn = fn
        o.is_dma = eng == "dma"
        o.queue = (queue or "sp") if o.is_dma else eng
        o.signal = False
        deps = set()
        for r in reads:
            if r.w is not None:
                deps.add(r.w)
            if r.psum:
                deps.update(j for j in r.r if self.ops[j].eng != eng)
        for r in writes:
            if r.w is not None:
                deps.add(r.w)
            deps.update(r.r)
        for r in reads:
            r.r.append(o.idx)
        for r in writes:
            r.w = o.idx
            r.r = []
        deps.discard(o.idx)
        o.deps = deps
        self.ops.append(o)
        return o

    def emit(self, final=False):
        nc = self.nc
        ops = self.ops
        for o in ops:
            for d in o.deps:
                od = ops[d]
                if od.eng == "pe" and o.eng == "pe":
                    continue
                od.signal = True
        last = {}
        for o in ops:
            if not o.is_dma:
                last[o.eng] = o
        for o in last.values():
            o.signal = True
        dlast = [None] * N_DMA_SEMS
        for o in ops:
            if o.is_dma:
                k = self.nd % N_DMA_SEMS
                self.nd += 1
                o.sem = k
                if dlast[k] is not None:
                    o.deps.add(dlast[k])
                dlast[k] = o.idx
                self.dcnt[k] += 16
                o.cnt = self.dcnt[k]
                o.signal = True
            elif o.signal:
                self.cnt[o.eng] += 1
                o.cnt = self.cnt[o.eng]
            else:
                o.cnt = None
        streams = {}
        for o in ops:
            streams.setdefault(o.queue, []).append(o)
        sems, dsems = self.sems, self.dsems
        base = dict(self.base)
        endvals = {("c", e): self.cnt[e] for e in COMPUTE}
        endvals.update({("d", k): self.dcnt[k] for k in range(N_DMA_SEMS)})

        def semof(key):
            return dsems[key[1]] if key[0] == "d" else sems[key[1]]

        def run_stream(engobj, qname):
            known = {}
            for key, v in base.items():
                if v > 0 and not (key == ("c", qname)):
                    engobj.wait_ge(semof(key), v)
                known[key] = v
            for o in streams.get(qname, []):
                need = {}
                for d in o.deps:
                    od = ops[d]
                    if od.eng == "pe" and o.eng == "pe":
                        continue
                    key = ("d", od.sem) if od.is_dma else ("c", od.eng)
                    if need.get(key, 0) < od.cnt:
                        need[key] = od.cnt
                for key, v in need.items():
                    if known.get(key, 0) >= v:
                        continue
                    known[key] = v
                    engobj.wait_ge(semof(key), v)
                ins = o.fn(engobj)
                if o.signal:
                    if o.is_dma:
                        ins.then_inc(dsems[o.sem], 16)
                    else:
                        ins.then_inc(sems[o.eng], 1)
            if final and qname == "sp":
                for key, v in endvals.items():
                    if v > known.get(key, 0):
                        engobj.wait_ge(semof(key), v)

        with nc.Block() as block:
            @block.sync
            def _(e):
                run_stream(e, "sp")

            @block.scalar
            def _(e):
                run_stream(e, "act")

            @block.vector
            def _(e):
                run_stream(e, "dve")

            @block.gpsimd
            def _(e):
                run_stream(e, "pool")

            @block.tensor
            def _(e):
                run_stream(e, "pe")
        self.base = endvals
        nops = len(ops)
        self.ops = []
        for r in self.res:
            r.w = None
            r.r = []
        return nops


class Ctx:
    pass


def make_consts(P, nc, es):
    c = Ctx()
    c.ones_bf = es.enter_context(nc.sbuf_tensor("c_ones_bf", [128, 128], BF16))
    c.ident_bf = es.enter_context(nc.sbuf_tensor("c_ident_bf", [128, 128], BF16))
    c.ones_f = es.enter_context(nc.sbuf_tensor("c_ones_f", [128, 128], F32))
    c.ident_f = es.enter_context(nc.sbuf_tensor("c_ident_f", [128, 128], F32))
    return c


def init_consts(P, c):
    r = P.R()
    P.op("pool", lambda e: e.memset(c.ones_bf[:], 1.0), writes=[r])
    P.op("pool", lambda e: e.memset(c.ones_f[:], 1.0), writes=[r])
    P.op("pool", lambda e: e.affine_select(out=c.ident_bf[:], in_=c.ones_bf[:], pattern=[[-1, 128]],
                                           compare_op=ALU.is_equal, fill=0.0, base=0, channel_multiplier=1),
         reads=[r], writes=[r])
    P.op("pool", lambda e: e.affine_select(out=c.ident_f[:], in_=c.ones_f[:], pattern=[[-1, 128]],
                                           compare_op=ALU.is_equal, fill=0.0, base=0, channel_multiplier=1),
         reads=[r], writes=[r])
    c.r = r


def rr(engs):
    i = [0]

    def nxt():
        e = engs[i[0] % len(engs)]
        i[0] += 1
        return e
    return nxt


def scaled_cast(P, eng, out_ap, in_ap, scale_ap, reads, writes):
    if eng == "act":
        if scale_ap is None:
            P.op("act", lambda e: e.copy(out=out_ap, in_=in_ap), reads=reads, writes=writes)
        else:
            P.op("act", lambda e: e.activation(out=out_ap, in_=in_ap, func=AF.Copy, scale=scale_ap), reads=reads, writes=writes)
    else:
        if scale_ap is None:
            P.op(eng, lambda e: e.tensor_copy(out=out_ap, in_=in_ap), reads=reads, writes=writes)
        else:
            P.op(eng, lambda e: e.tensor_scalar(out=out_ap, in0=in_ap, scalar1=scale_ap, scalar2=None, op0=ALU.mult), reads=reads, writes=writes)


def load_gain_cols(P, nc, es, name, g_ap, nk):
    t = es.enter_context(nc.sbuf_tensor(name, [128, nk], F32))
    r = P.R()
    P.op("dma", lambda e: e.dma_start(out=t[:], in_=g_ap.rearrange("(k p) -> p k", p=128), allow_slow_non_contiguous=True), writes=[r])
    return t, r


def ffn_phase(P, nc, C, tag, src, dst, g_ap, wg_ap, wu_ap, wd_ap, ntok=NTOK):
    T = 256
    NS = T // 128
    ntiles = ntok // T
    es = ExitStack()

    def SB(name, shape, dt=F32):
        return es.enter_context(nc.sbuf_tensor(tag + name, shape, dt))

    def PS(name, shape, dt=F32):
        return es.enter_context(nc.psum_tensor(tag + name, shape, dt))

    wg = SB("wg", [128, 8, DFF], BF16)
    wu = SB("wu", [128, 8, DFF], BF16)
    wd = SB("wd", [128, NFF, D], BF16)
    NPC = 8
    PW = DFF // NPC
    r_wg = [P.R() for _ in range(NPC)]
    r_wu = [P.R() for _ in range(NPC)]
    r_wd = [P.R() for _ in range(NFF)]
    NSTG = 3
    stg = [SB("stg%d" % i, [128, DFF], F32) for i in range(NSTG)]
    r_stg = [P.R() for _ in range(NSTG)]
    g_b = SB("g_b", [128, D], F32)
    r_g = P.R()

    xt = [SB("xt%d" % i, [128, NS, D], F32) for i in range(2)]
    r_xt = [P.R() for _ in range(2)]
    hb = SB("hb", [128, NS, D], BF16)
    r_hb = P.R()
    hT = SB("hT", [128, 8, T], BF16)
    r_hT = P.R()
    aT = SB("aT", [128, NFF, T], BF16)
    r_aT = [P.R() for _ in range(NFF)]
    sg = [SB("sg%d" % i, [128, T], F32) for i in range(2)]
    r_sg = [P.R() for _ in range(2)]
    ss = SB("ss", [128, 2 * NS], F32)
    r_ss = P.R()

    ptr = [PS("ptr%d" % i, [128, 1024], BF16) for i in range(2)]
    r_ptr = [P.RP() for _ in range(2)]
    pgu = [PS("pgu%d" % i, [128, 512], F32) for i in range(2)]
    r_pgu = [P.RP() for _ in range(2)]
    pdn = [PS("pdn%d" % i, [128, 512], F32) for i in range(4)]
    r_pdn = [P.RP() for _ in range(4)]

    src_t = src.rearrange("(n s p) d -> n p s d", p=128, s=NS)
    dst_t = dst.rearrange("(n s p) d -> n p s d", p=128, s=NS)

    def load_x(i):
        b = i % 2
        P.op("dma", lambda e: e.dma_start(out=xt[b][:], in_=src_t[i]), writes=[r_xt[b]])

    load_x(0)
    if ntiles > 1:
        load_x(1)

    P.op("dma", lambda e: e.dma_start(out=g_b[:], in_=g_ap.partition_broadcast(128)), writes=[r_g])
    cv = rr(["dve", "act"])
    si = [0]

    def load_w(src_ap3, out_ap3, rres):
        s_ = si[0] % NSTG
        si[0] += 1
        a, b2 = src_ap3.shape[1], src_ap3.shape[2]
        st_ap = stg[s_][:, 0:a * b2].rearrange("p (a b) -> p a b", a=a)
        P.op("dma", lambda e: e.dma_start(out=st_ap, in_=src_ap3), writes=[r_stg[s_]])
        scaled_cast(P, cv(), out_ap3, st_ap, None, [r_stg[s_]], rres)

    wg3 = wg_ap.rearrange("(k p) n -> p k n", p=128)
    wu3 = wu_ap.rearrange("(k p) n -> p k n", p=128)
    ev = rr(["act", "dve"])

    def emit_norm(i):
        b = i % 2
        x, rx = xt[b], r_xt[b]
        for s in range(NS):
            P.op("act", lambda e, s=s, x=x: e.activation(out=hb[:, s, :], in_=x[:, s, :], func=AF.Square, accum_out=ss[:, s:s + 1]),
                 reads=[rx], writes=[r_hb, r_ss])
        P.op("act", lambda e: e.activation(out=ss[:, NS:2 * NS], in_=ss[:, 0:NS], func=AF.Sqrt, bias=RMS_EPS, scale=1.0 / D),
             reads=[r_ss], writes=[r_ss])
        P.op("dve", lambda e: e.reciprocal(out=ss[:, NS:2 * NS], in_=ss[:, NS:2 * NS]), reads=[r_ss], writes=[r_ss])
        for s in range(NS):
            P.op("dve", lambda e, s=s, x=x: e.scalar_tensor_tensor(out=hb[:, s, :], in0=x[:, s, :], scalar=ss[:, NS + s:NS + s + 1], in1=g_b[:], op0=ALU.mult, op1=ALU.mult),
                 reads=[rx, r_ss, r_g], writes=[r_hb])
        for half in range(2):
            pt = ptr[half]
            rp = r_ptr[half]

            def tr(e, half=half, pt=pt):
                ins = None
                for kk in range(4):
                    k = half * 4 + kk
                    for s in range(NS):
                        ins = e.transpose(out=pt[:, kk * T + s * 128: kk * T + (s + 1) * 128], in_=hb[:, s, k * 128:(k + 1) * 128], identity=C.ident_bf[:])
                return ins
            P.op("pe", tr, reads=[r_hb, C.r], writes=[rp])
            scaled_cast(P, ev(), hT[:, half * 4:(half + 1) * 4, :], pt[:, 0:4 * T].rearrange("p (k t) -> p k t", k=4), None, [rp], [r_hT])

    def emit_gu(i, c):
        pb = c % 2
        pg = pgu[pb]

        def gu(e, c=c, pg=pg):
            ins = None
            for k in range(8):
                ins = e.matmul(pg[:, 0:T], lhsT=wg[:, k, c * 128:(c + 1) * 128], rhs=hT[:, k, :], start=(k == 0), stop=(k == 7))
            for k in range(8):
                ins = e.matmul(pg[:, T:2 * T], lhsT=wu[:, k, c * 128:(c + 1) * 128], rhs=hT[:, k, :], start=(k == 0), stop=(k == 7))
            return ins
        pcs = sorted(set([(c * 128) // PW, (c * 128 + 127) // PW]))
        P.op("pe", gu, reads=[r_hT] + [r_wg[j] for j in pcs] + [r_wu[j] for j in pcs], writes=[r_pgu[pb]])
        P.op("act", lambda e, pg=pg, pb=pb: e.activation(out=sg[pb][:], in_=pg[:, 0:T], func=AF.Silu), reads=[r_pgu[pb]], writes=[r_sg[pb]])
        P.op("dve", lambda e, pg=pg, pb=pb, c=c: e.tensor_tensor(out=aT[:, c, :], in0=pg[:, T:2 * T], in1=sg[pb][:], op=ALU.mult),
             reads=[r_pgu[pb], r_sg[pb]], writes=[r_aT[c]])

    def emit_down(i):
        b = i % 2
        x, rx = xt[b], r_xt[b]
        for s in range(NS):
            for h in range(2):
                q = (s * 2 + h) % 4
                pd = pdn[q]

                def dn(e, s=s, h=h, pd=pd):
                    ins = None
                    for c in range(NFF):
                        ins = e.matmul(pd[:], lhsT=aT[:, c, s * 128:(s + 1) * 128], rhs=wd[:, c, h * 512:(h + 1) * 512], start=(c == 0), stop=(c == NFF - 1))
                    return ins
                P.op("pe", dn, reads=r_aT + r_wd, writes=[r_pdn[q]])
                P.op("dve", lambda e, s=s, h=h, pd=pd, x=x: e.scalar_tensor_tensor(out=x[:, s, h * 512:(h + 1) * 512], in0=pd[:], scalar=0.5, in1=x[:, s, h * 512:(h + 1) * 512],
                                                                              op0=ALU.mult, op1=ALU.add),
                     reads=[r_pdn[q], rx], writes=[rx])
        P.op("dma", lambda e, i=i, x=x: e.dma_start(out=dst_t[i], in_=x[:]), reads=[rx])
        if i + 2 < ntiles:
            load_x(i + 2)

    emit_norm(0)
    done_c = 0
    for j in range(NPC):
        load_w(wg3[:, :, j * PW:(j + 1) * PW], wg[:, :, j * PW:(j + 1) * PW], [r_wg[j]])
        load_w(wu3[:, :, j * PW:(j + 1) * PW], wu[:, :, j * PW:(j + 1) * PW], [r_wu[j]])
        while done_c < NFF and (done_c * 128 + 127) // PW <= j:
            emit_gu(0, done_c)
            done_c += 1
    for c in range(0, NFF, 2):
        load_w(wd_ap[c * 128:(c + 2) * 128, :].rearrange("(c p) n -> p c n", p=128), wd[:, c:c + 2, :], [r_wd[c], r_wd[c + 1]])
    emit_down(0)
    for i in range(1, ntiles):
        emit_norm(i)
        for c in range(NFF):
            emit_gu(i, c)
        emit_down(i)
    n = P.emit()
    es.close()
    return n


class TileNorm:
    NPTR = 2

    def __init__(self, P, nc, C, SB, PS, T, nxt=1):
        self.P, self.C, self.T = P, C, T
        NS = T // 128
        self.NS = NS
        self.xt = [SB("xt%d" % i, [128, NS, D], F32) for i in range(nxt)]
        self.r_xt = [P.R() for _ in range(nxt)]
        self.hb = SB("hb", [128, NS, D], BF16)
        self.r_hb = P.R()
        self.hT = SB("hT", [128, 8, T], BF16)
        self.r_hT = P.R()
        self.ss = SB("ss", [128, 2 * NS], F32)
        self.r_ss = P.R()
        nptr = getattr(self, "NPTR", 2) if False else TileNorm.NPTR
        self.ptr = [PS("ptr%d" % i, [128, 1024], BF16) for i in range(nptr)]
        self.r_ptr = [P.RP() for _ in range(nptr)]
        if nptr == 1:
            self.ptr = self.ptr * 2
            self.r_ptr = self.r_ptr * 2
        self.ev = rr(["act", "dve"])

    def load(self, src_t, i):
        b = i % len(self.xt)
        xt, r = self.xt[b], self.r_xt[b]
        self.P.op("dma", lambda e: e.dma_start(out=xt[:], in_=src_t[i]), writes=[r])

    def norm(self, i):
        P, C, T, NS = self.P, self.C, self.T, self.NS
        b = i % len(self.xt)
        x, rx = self.xt[b], self.r_xt[b]
        ss, hb, hT = self.ss, self.hb, self.hT
        for s in range(NS):
            P.op("act", lambda e, s=s: e.activation(out=hb[:, s, :], in_=x[:, s, :], func=AF.Square, accum_out=ss[:, s:s + 1]),
                 reads=[rx], writes=[self.r_hb, self.r_ss])
        P.op("act", lambda e: e.activation(out=ss[:, NS:2 * NS], in_=ss[:, 0:NS], func=AF.Sqrt, bias=RMS_EPS, scale=1.0 / D),
             reads=[self.r_ss], writes=[self.r_ss])
        P.op("dve", lambda e: e.reciprocal(out=ss[:, NS:2 * NS], in_=ss[:, NS:2 * NS]), reads=[self.r_ss], writes=[self.r_ss])
        for s in range(NS):
            P.op("dve", lambda e, s=s: e.tensor_scalar(out=hb[:, s, :], in0=x[:, s, :], scalar1=ss[:, NS + s:NS + s + 1], scalar2=None, op0=ALU.mult),
                 reads=[rx, self.r_ss], writes=[self.r_hb])
        for half in range(2):
            pt, rp = self.ptr[half], self.r_ptr[half]

            def tr(e, half=half, pt=pt):
                ins = None
                for kk in range(4):
                    k = half * 4 + kk
                    for s in range(NS):
                        ins = e.transpose(out=pt[:, kk * T + s * 128: kk * T + (s + 1) * 128], in_=hb[:, s, k * 128:(k + 1) * 128], identity=C.ident_bf[:])
                return ins
            P.op("pe", tr, reads=[self.r_hb, C.r], writes=[rp])
            scaled_cast(P, self.ev(), hT[:, half * 4:(half + 1) * 4, :], pt[:, 0:4 * T].rearrange("p (k t) -> p k t", k=4), None, [rp], [self.r_hT])
        return x, rx


def load_weight_cols(P, nc, SB, name, w_ap, c0, c1, gcol, r_g, piece=832):
    n = c1 - c0
    wt = SB(name, [128, 8, n], BF16)
    rs = [P.R() for _ in range(8)]
    stg = [SB(name + "_stg%d" % i, [128, piece], F32) for i in range(2)]
    r_stg = [P.R() for _ in range(2)]
    cv = rr(["dve", "act"])
    si = 0
    for k in range(8):
        for p0 in range(0, n, piece):
            p1 = min(n, p0 + piece)
            s = si % 2
            si += 1
            st_ap = stg[s][:, 0:p1 - p0]
            P.op("dma", lambda e, st_ap=st_ap, k=k, p0=p0, p1=p1: e.dma_start(out=st_ap, in_=w_ap[k * 128:(k + 1) * 128, c0 + p0:c0 + p1]), writes=[r_stg[s]])
            scaled_cast(P, cv(), wt[:, k, p0:p1], st_ap, None if gcol is None else gcol[:, k:k + 1], [r_stg[s]] + ([r_g] if r_g else []), [rs[k]])
    return wt, rs


def head_norm(P, SB_tmp, ps, rps, gain_b, out_ap, writes, n_heads=8, eps=RMS_EPS, tmp=None):
    junk, r_junk, ssq, r_ssq, t32, r_t32 = tmp
    P.op("act", lambda e: e.activation(out=junk[:], in_=ps, func=AF.Square), reads=rps, writes=[r_junk])
    P.op("dve", lambda e: e.tensor_reduce(out=ssq[:, 0:n_heads], in_=junk[:].rearrange("p (h d) -> p h d", d=64), axis=AX.X, op=ALU.add), reads=[r_junk], writes=[r_ssq])
    P.op("act", lambda e: e.activation(out=ssq[:, 8:8 + n_heads], in_=ssq[:, 0:n_heads], func=AF.Sqrt, bias=eps, scale=1.0 / 64), reads=[r_ssq], writes=[r_ssq])
    P.op("dve", lambda e: e.reciprocal(out=ssq[:, 8:8 + n_heads], in_=ssq[:, 8:8 + n_heads]), reads=[r_ssq], writes=[r_ssq])
    P.op("dve", lambda e: e.tensor_tensor(out=t32[:].rearrange("p (h d) -> p h d", d=64), in0=ps.rearrange("p (h d) -> p h d", d=64),
                                           in1=ssq[:, 8:8 + n_heads].unsqueeze(2).to_broadcast([128, n_heads, 64]), op=ALU.mult),
         reads=rps + [r_ssq], writes=[r_t32])
    P.op("pool", lambda e: e.tensor_tensor(out=out_ap.rearrange("p (h d) -> p h d", d=64), in0=t32[:].rearrange("p (h d) -> p h d", d=64), in1=gain_b, op=ALU.mult),
         reads=[r_t32], writes=writes)


def attn_phase(P, nc, C, tag, src, osb, W, ntok=NTOK):
    T = 256
    NS = 2
    ntiles = ntok // T
    TPS = SEQ // T
    es = ExitStack()

    def SB(name, shape, dt=F32):
        return es.enter_context(nc.sbuf_tensor(tag + name, shape, dt))

    def PS(name, shape, dt=F32):
        return es.enter_context(nc.psum_tensor(tag + name, shape, dt))

    ND = int(os.environ.get("ATTN_ND", "0"))
    TileNorm.NPTR = 1 if ND > 0 else 2
    TN = TileNorm(P, nc, C, SB, PS, T, nxt=2)
    TileNorm.NPTR = 2
    if ND > 0:
        pdum = PS("pdum", [128, 512], F32)
        dones = SB("dones", [128, 512], BF16)
        r_dones = P.R()
        P.op("pool", lambda e: e.memset(dones[:], 1.0), writes=[r_dones])
        r_pdum = P.RP()

        def warm(k=ND):
            def f(e):
                ins = None
                for _ in range(k):
                    ins = e.matmul(pdum[:], lhsT=C.ident_bf[:], rhs=dones[:], start=True, stop=True)
                return ins
            P.op("pe", f, reads=[r_dones, C.r], writes=[r_pdum])
    else:
        def warm(k=0):
            pass
    src_t = src.rearrange("(n s p) d -> n p s d", p=128, s=NS)
    osb_t = osb.rearrange("(n s p) d -> n p s d", p=128, s=NS)
    TN.load(src_t, 0)
    gcol, r_g = load_gain_cols(P, nc, es, tag + "gcol", W["norm_mix"], 8)
    wqkv, r_w = load_weight_cols(P, nc, SB, "wqkv", W["w_in"], 0, 1536, gcol, r_g, piece=768)
    r_c = P.R()
    gq_b = SB("gq_b", [128, 64], F32)
    gk_b = SB("gk_b", [128, 64], F32)
    go_b = SB("go_b", [128, 512], F32)
    P.op("dma", lambda e: e.dma_start(out=gq_b[:], in_=W["sb_q_norm"].partition_broadcast(128)), writes=[r_c])
    P.op("dma", lambda e: e.dma_start(out=gk_b[:], in_=W["sb_k_norm"].partition_broadcast(128)), writes=[r_c])
    P.op("dma", lambda e: e.dma_start(out=go_b[:], in_=W["sb_out_norm"].partition_broadcast(128)), writes=[r_c])
    P.op("pool", lambda e: e.tensor_scalar(out=gq_b[:], in0=gq_b[:], scalar1=0.125, scalar2=None, op0=ALU.mult), reads=[r_c], writes=[r_c])
    tri = SB("tri", [128, 128], BF16)
    P.op("pool", lambda e: e.affine_select(out=tri[:], in_=C.ones_bf[:], pattern=[[-1, 128]], compare_op=ALU.is_ge, fill=0.0, base=0, channel_multiplier=1),
         reads=[C.r], writes=[r_c])

    KT = SB("KT", [128, 4, SEQ], BF16)
    r_KT = [P.R() for _ in range(SEQ // 128)]
    Vb = SB("Vb", [128, SEQ // 128, 512], BF16)
    r_Vb = [P.R() for _ in range(SEQ // 128)]
    qT = SB("qT", [128, 4, T], BF16)
    r_qT = P.R()
    qn = [SB("qn%d" % i, [128, 512], BF16) for i in range(2)]
    r_qn = [P.R() for _ in range(2)]
    hn_tmp = (SB("hn_junk", [128, 512], F32), P.R(), SB("hn_ssq", [128, 16], F32), P.R(), SB("hn_t32", [128, 512], F32), P.R())
    ob = SB("ob", [128, NS, 512], BF16)
    r_ob = P.R()

    pz2 = PS("pz2", [128, 1024], F32)
    pproj = [pz2[:, 0:512], pz2[:, 512:1024]]
    r_pproj = [P.RP() for _ in range(2)]
    pcs = [PS("pc%d" % i, [128, 512], F32) for i in range(2)]
    r_pc = [P.RP() for _ in range(2)]
    po = [PS("po%d" % i, [128, 512], F32) for i in range(2)]
    r_po = [P.RP() for _ in range(2)]

    NE = 5
    NL = 4
    NA = 5
    Eb = [SB("E%d" % i, [128, 2, T], F32) for i in range(NE)]
    r_E = [P.R() for _ in range(NE)]
    Lb = [SB("L%d" % i, [128, 2, T], BF16) for i in range(NL)]
    r_L = [P.R() for _ in range(NL)]
    Ls = [SB("Ls%d" % i, [128, 2, T], BF16) for i in range(2)]
    r_Ls = [P.R() for _ in range(2)]
    Xb = [SB("X%d" % i, [128, 2, T], F32) for i in range(2)]
    r_X = [P.R() for _ in range(2)]
    At = [SB("At%d" % i, [128, 2, T], BF16) for i in range(NA)]
    r_At = [P.R() for _ in range(NA)]

    pj = [0]
    for i in range(ntiles):
        g = i % TPS
        q0 = g * T
        x, rx = TN.norm(i)
        if i + 1 < ntiles:
            TN.load(src_t, i + 1)
        hT, r_hT = TN.hT, TN.r_hT
        for which in range(3):
            for s in range(NS):
                b = pj[0] % 2
                pj[0] += 1
                pp, rpp = pproj[b], r_pproj[b]

                def proj(e, which=which, s=s, pp=pp):
                    ins = None
                    for k in range(8):
                        ins = e.matmul(pp, lhsT=hT[:, k, s * 128:(s + 1) * 128], rhs=wqkv[:, k, which * 512:(which + 1) * 512], start=(k == 0), stop=(k == 7))
                    return ins
                P.op("pe", proj, reads=[r_hT] + r_w, writes=[rpp])
                blk = q0 // 128 + s
                if which == 2:
                    P.op("act", lambda e, pp=pp, blk=blk: e.copy(out=Vb[:, blk, :], in_=pp), reads=[rpp], writes=[r_Vb[blk]])
                    continue
                qb = (which * NS + s) % 2
                gain = (gq_b if which == 0 else gk_b)[:].unsqueeze(1).to_broadcast([128, 8, 64])
                head_norm(P, None, pp, [rpp, r_c], gain, qn[qb][:], [r_qn[qb]], tmp=hn_tmp)
                ptb = TN.ptr[qb]
                rptb = TN.r_ptr[qb]

                def trq(e, qb=qb, ptb=ptb):
                    ins = None
                    for hp in range(4):
                        ins = e.transpose(out=ptb[:, hp * 128:(hp + 1) * 128], in_=qn[qb][:, hp * 128:(hp + 1) * 128], identity=C.ident_bf[:])
                    return ins
                P.op("pe", trq, reads=[r_qn[qb], C.r], writes=[rptb])
                srcv = ptb[:, 0:512].rearrange("p (h t) -> p h t", h=4)
                if which == 0:
                    P.op("dve", lambda e, srcv=srcv, s=s: e.tensor_copy(out=qT[:, :, s * 128:(s + 1) * 128], in_=srcv), reads=[rptb], writes=[r_qT])
                else:
                    P.op("dve", lambda e, srcv=srcv, blk=blk: e.tensor_copy(out=KT[:, :, blk * 128:(blk + 1) * 128], in_=srcv), reads=[rptb], writes=[r_KT[blk]])
        nkb = 2 * g + 2
        units = [(hp, kb) for hp in range(4) for kb in range(nkb - 1, -1, -1)]
        NU = len(units)
        zv = pz2[:].rearrange("p (b w) -> p b w", b=2)[:, :, 0:T]

        def stage1(u):
            hp, kb = units[u]
            eb = u % NE
            lb = u % NL

            def zmm(e):
                e.matmul(pz2[:, 0:T], lhsT=KT[0:64, hp, kb * 128:(kb + 1) * 128], rhs=qT[0:64, hp, :], start=True, stop=True)
                return e.matmul(pz2[:, 512:512 + T], lhsT=KT[64:128, hp, kb * 128:(kb + 1) * 128], rhs=qT[64:128, hp, :], start=True, stop=True)
            P.op("pe", zmm, reads=[r_KT[kb], r_qT], writes=r_pproj)
            P.op("act", lambda e: e.activation(out=Eb[eb][:], in_=zv, func=AF.Exp), reads=r_pproj, writes=[r_E[eb]])
            if kb >= 2 * g:
                base = q0 - kb * 128
                P.op("pool", lambda e: e.affine_select(out=Eb[eb][:], in_=Eb[eb][:], pattern=[[0, 2], [1, T]], compare_op=ALU.is_gt, fill=0.0, base=base, channel_multiplier=-1),
                     reads=[r_E[eb]], writes=[r_E[eb]])
            P.op("act", lambda e: e.activation(out=Lb[lb][:], in_=Eb[eb][:], func=AF.Ln, bias=1.0), reads=[r_E[eb]], writes=[r_L[lb]])

        def stage2(u):
            hp, kb = units[u]
            cb = u % 2
            eb = u % NE
            lb = u % NL
            xb = u % 2
            ab = u % NA
            lsb = hp % 2
            first = kb == nkb - 1
            cap = pcs[cb][:, :]
            L2 = Lb[lb][:].rearrange("p b t -> p (b t)")
            S2 = Ls[lsb][:].rearrange("p b t -> p (b t)")

            def cmm(e):
                ins = e.matmul(cap, lhsT=tri[:], rhs=L2, start=True, stop=first)
                if not first:
                    ins = e.matmul(cap, lhsT=C.ones_bf[:], rhs=S2, start=False, stop=True)
                return ins
            P.op("pe", cmm, reads=[r_L[lb], r_c, C.r] + ([] if first else [r_Ls[lsb]]), writes=[r_pc[cb]])
            if kb > 0:
                if first:
                    P.op("pool", lambda e: e.tensor_copy(out=Ls[lsb][:], in_=Lb[lb][:]), reads=[r_L[lb]], writes=[r_Ls[lsb]])
                else:
                    P.op("pool", lambda e: e.tensor_tensor(out=Ls[lsb][:], in0=Ls[lsb][:], in1=Lb[lb][:], op=ALU.add), reads=[r_L[lb], r_Ls[lsb]], writes=[r_Ls[lsb]])
            P.op("act", lambda e: e.activation(out=Xb[xb][:].rearrange("p b t -> p (b t)"), in_=cap, func=AF.Exp, scale=-1.0), reads=[r_pc[cb]], writes=[r_X[xb]])
            P.op("dve", lambda e: e.tensor_tensor(out=At[ab][:], in0=Eb[eb][:], in1=Xb[xb][:], op=ALU.mult), reads=[r_E[eb], r_X[xb]], writes=[r_At[ab]])

        def stage3(u):
            hp, kb = units[u]
            ab = u % NA

            def omm(e, nkb=nkb):
                ins = None
                for hh in range(2):
                    h = 2 * hp + hh
                    for s in range(NS):
                        if kb == nkb - 1 and s == 0:
                            continue
                        st = ((kb == nkb - 1) if s == 1 else (kb == nkb - 2)) and h == 0
                        ins = e.matmul(po[s][:, h * 64:(h + 1) * 64], lhsT=At[ab][:, hh, s * 128:(s + 1) * 128], rhs=Vb[:, kb, h * 64:(h + 1) * 64], start=st, stop=(kb == 0),
                                       skip_group_check=True)
                return ins
            P.op("pe", omm, reads=[r_At[ab], r_Vb[kb]], writes=r_po)
            warm()

        lvl = int(os.environ.get("ATTN_LVL", "9"))
        SK = 2
        for n in range(NU + 2 * SK):
            if n < NU and lvl >= 1:
                stage1(n)
            if 0 <= n - SK < NU and lvl >= 2:
                stage2(n - SK)
            if 0 <= n - 2 * SK < NU and lvl >= 3:
                stage3(n - 2 * SK)
        if lvl < 4:
            continue
        for s in range(NS):
            head_norm(P, None, po[s][:], [r_po[s], r_c], go_b[:].rearrange("p (h d) -> p h d", d=64), ob[:, s, :], [r_ob], tmp=hn_tmp)
        if lvl < 5:
            continue
        P.op("dma", lambda e, i=i: e.dma_start(out=osb_t[i], in_=ob[:]), reads=[r_ob])
    n = P.emit()
    es.close()
    return n


DEC = 0.6065306597126334


def load_cols(P, nc, SB, name, v_ap, nk, r):
    t = SB(name, [128, nk], F32)
    P.op("dma", lambda e: e.dma_start(out=t[:], in_=v_ap.rearrange("(k p) -> p k", p=128), allow_slow_non_contiguous=True), writes=[r])
    return t


def rwkv_phase(P, nc, C, tag, src, dst, osb, W, ntok=NTOK):
    T = 256
    NS = 2
    NCH = 4
    ntiles = ntok // T
    TPS = SEQ // T
    es = ExitStack()

    def SB(name, shape, dt=F32):
        return es.enter_context(nc.sbuf_tensor(tag + name, shape, dt))

    def PS(name, shape, dt=F32):
        return es.enter_context(nc.psum_tensor(tag + name, shape, dt))

    TN = TileNorm(P, nc, C, SB, PS, T, nxt=2)
    src_t = src.rearrange("(n s p) d -> n p s d", p=128, s=NS)
    dst_t = dst.rearrange("(n s p) d -> n p s d", p=128, s=NS)
    osb_t = osb.rearrange("(n s p) d -> n p s d", p=128, s=NS)
    TN.load(src_t, 0)
    gcol, r_g = load_gain_cols(P, nc, es, tag + "gcol", W["norm_mix"], 8)
    wrw, r_wrw = load_weight_cols(P, nc, SB, "wrw", W["w_in"], 1536, 3328, gcol, r_g, piece=896)
    wout, r_wout = load_weight_cols(P, nc, SB, "wout", W["w_out"], 0, 1024, None, None, piece=1024)
    r_c = P.R()
    mu = load_cols(P, nc, SB, "mu", W["rw_mu"], 14, r_c)
    omm = SB("omm", [128, 14], F32)
    P.op("dve", lambda e: e.tensor_scalar(out=omm[:], in0=mu[:], scalar1=-1.0, scalar2=1.0, op0=ALU.mult, op1=ALU.add), reads=[r_c], writes=[r_c])
    w0c = load_cols(P, nc, SB, "w0c", W["rw_w0"], 4, r_c)
    a0c = load_cols(P, nc, SB, "a0c", W["rw_a0"], 4, r_c)
    kkc = load_cols(P, nc, SB, "kkc", W["rw_k_k"], 4, r_c)
    kac = load_cols(P, nc, SB, "kac", W["rw_k_a"], 4, r_c)
    rkc = load_cols(P, nc, SB, "rkc", W["rw_r_k"], 4, r_c)
    okac = SB("okac", [128, 4], F32)
    P.op("dve", lambda e: e.tensor_scalar(out=okac[:], in0=kac[:], scalar1=-1.0, scalar2=1.0, op0=ALU.mult, op1=ALU.add), reads=[r_c], writes=[r_c])
    stg = SB("cstg", [128, 512], F32)
    wa2 = SB("wa2", [128, 512], BF16)
    g2b = SB("g2b", [128, 512], BF16)
    P.op("dma", lambda e: e.dma_start(out=stg[0:64, :], in_=W["rw_w2"][:, :]), writes=[r_c])
    P.op("dma", lambda e: e.dma_start(out=stg[64:128, :], in_=W["rw_a2"][:, :]), writes=[r_c])
    P.op("dve", lambda e: e.tensor_copy(out=wa2[:], in_=stg[:]), reads=[r_c], writes=[r_c])
    P.op("dma", lambda e: e.dma_start(out=stg[:], in_=W["rw_g2"][:, :]), reads=[r_c], writes=[r_c])
    P.op("dve", lambda e: e.tensor_copy(out=g2b[:], in_=stg[:]), reads=[r_c], writes=[r_c])
    lnw_b = SB("lnw_b", [64, 512], F32)
    lnb_b = SB("lnb_b", [64, 512], F32)
    P.op("dma", lambda e: e.dma_start(out=lnw_b[:], in_=W["rw_ln_w"].partition_broadcast(64)), writes=[r_c])
    P.op("dma", lambda e: e.dma_start(out=lnb_b[:], in_=W["rw_ln_b"].partition_broadcast(64)), writes=[r_c])
    RKbd = SB("RKbd", [128, 4, 2], BF16)
    P.op("pool", lambda e: e.memset(RKbd[:], 0.0), writes=[r_c])
    P.op("pool", lambda e: e.tensor_copy(out=RKbd[0:64, :, 0], in_=rkc[0:64, :]), reads=[r_c], writes=[r_c])
    P.op("pool", lambda e: e.tensor_copy(out=RKbd[64:128, :, 1], in_=rkc[64:128, :]), reads=[r_c], writes=[r_c])
    BD = SB("BD", [128, 128], BF16)
    P.op("pool", lambda e: e.memset(BD[:], 0.0), writes=[r_c])
    P.op("pool", lambda e: e.memset(BD[0:64, 0:64], 1.0), writes=[r_c])
    P.op("pool", lambda e: e.memset(BD[64:128, 64:128], 1.0), writes=[r_c])
    I2 = SB("I2", [128, 64], BF16)
    P.op("pool", lambda e: e.tensor_tensor(out=I2[:], in0=C.ident_bf[:, 0:64], in1=C.ident_bf[:, 64:128], op=ALU.add), reads=[C.r], writes=[r_c])
    mSL = SB("mSL", [64, 64], BF16)
    mSU = SB("mSU", [64, 64], BF16)
    mIU = SB("mIU", [64, 64], BF16)
    mC = SB("mC", [128, 128], BF16)
    ob64 = C.ones_bf[0:64, 0:64]
    P.op("pool", lambda e: e.affine_select(out=mSL[:], in_=ob64, pattern=[[-1, 64]], compare_op=ALU.is_gt, fill=0.0, base=0, channel_multiplier=1), reads=[C.r], writes=[r_c])
    P.op("pool", lambda e: e.affine_select(out=mSU[:], in_=ob64, pattern=[[1, 64]], compare_op=ALU.is_gt, fill=0.0, base=0, channel_multiplier=-1), reads=[C.r], writes=[r_c])
    P.op("pool", lambda e: e.affine_select(out=mIU[:], in_=ob64, pattern=[[1, 64]], compare_op=ALU.is_ge, fill=0.0, base=0, channel_multiplier=-1), reads=[C.r], writes=[r_c])
    mA = SB("mA", [64, 192], BF16)
    mB = SB("mB", [64, 192], BF16)
    P.op("pool", lambda e: e.memset(mA[:], 1.0), writes=[r_c])
    P.op("pool", lambda e: e.memset(mB[:], 1.0), writes=[r_c])
    P.op("pool", lambda e: e.tensor_copy(out=mA[:, 0:64], in_=mSL[:]), reads=[r_c], writes=[r_c])
    P.op("pool", lambda e: e.tensor_copy(out=mA[:, 128:192], in_=mSL[:]), reads=[r_c], writes=[r_c])
    P.op("pool", lambda e: e.tensor_copy(out=mB[:, 0:64], in_=mIU[:]), reads=[r_c], writes=[r_c])
    P.op("pool", lambda e: e.tensor_copy(out=mB[:, 128:192], in_=mSU[:]), reads=[r_c], writes=[r_c])
    P.op("pool", lambda e: e.memset(mC[:], 1.0), writes=[r_c])
    P.op("pool", lambda e: e.tensor_copy(out=mC[0:64, 0:64], in_=mIU[:]), reads=[r_c], writes=[r_c])
    cm = SB("cm", [128, T], F32)
    P.op("pool", lambda e: e.memset(cm[:], 1.0), writes=[r_c])
    P.op("pool", lambda e: e.memset(cm[:].rearrange("p (c t) -> p c t", t=64)[:, :, 0:1], 0.0), writes=[r_c])

    U = SB("U", [128, 14, T + 2], F32)
    r_U = P.R()
    X = SB("X", [128, 14, T], F32)
    r_X = [P.R() for _ in range(14)]
    tsh = SB("tsh", [128, T], F32)
    r_tsh = P.R()
    twxa = SB("twxa", [128, T], BF16)
    r_twxa = P.R()
    sgg = SB("sgg", [128, T], BF16)
    r_sgg = P.R()
    NB = 6
    NSET = 2
    bbs = [[SB("b%d_%d" % (j, i), [128, T], F32) for i in range(NB)] for j in range(NSET)]
    r_bbs = [[P.R() for _ in range(NB)] for j in range(NSET)]
    sqbs = [SB("sqb%d" % j, [128, T], BF16) for j in range(NSET)]
    r_sqbs = [P.R() for j in range(NSET)]
    rkbs = [SB("rkb%d" % j, [128, T], BF16) for j in range(NSET)]
    r_rkbs = [P.R() for j in range(NSET)]
    ARD = [SB("ARD%d" % i, [128, NCH, 192], BF16) for i in range(4)]
    r_ARD = [P.R() for _ in range(4)]
    BKI = [SB("BKI%d" % i, [128, NCH, 192], BF16) for i in range(4)]
    r_BKI = [P.R() for _ in range(4)]
    for hp in range(4):
        P.op("pool", lambda e, hp=hp: e.tensor_copy(out=BKI[hp][:, :, 64:128], in_=I2[:].unsqueeze(1).to_broadcast([128, NCH, 64])), reads=[r_c], writes=[r_BKI[hp]])
    NSV = 6
    SV = [SB("SV%d" % i, [128, 8, 64], F32) for i in range(NSV)]
    r_SVv = [P.R() for _ in range(NSV)]
    r_SVs = [P.R() for _ in range(NSV)]
    Gt = SB("Gt", [64, NCH, 512], F32)
    r_Gt = [P.R() for _ in range(NCH)]
    bsc = SB("bsc", [64, NCH, 8], F32)
    r_bsc = P.R()
    ZPP = [[SB("ZPP%d_%d" % (g_, j), [64, 4, 256], BF16) for j in range(2)] for g_ in range(2)]
    r_Zb = [[P.R() for j in range(2)] for g_ in range(2)]
    r_PP = [[P.R() for j in range(2)] for g_ in range(2)]
    BRB = [SB("BRB%d" % g_, [64, 4, 192], BF16) for g_ in range(2)]
    r_BRB = [P.R() for _ in range(2)]
    LB = [SB("LB%d" % g_, [128, 4, 128], F32) for g_ in range(2)]
    r_LB = [P.R() for _ in range(2)]
    LBa = [SB("LBa%d" % g_, [128, 4, 128], F32) for g_ in range(2)]
    r_LBa = [P.R() for _ in range(2)]
    Yc = SB("Yc", [64, 512], F32)
    r_Yc = P.R()
    t1 = SB("t1", [64, 512], F32)
    r_t1 = P.R()
    t2 = SB("t2", [64, 512], F32)
    r_t2 = P.R()
    st = SB("st", [64, 48], F32)
    r_st = P.R()
    orw = SB("orw", [64, NCH, 512], BF16)
    r_orw = P.R()
    osbt = SB("osbt", [128, NS, 512], BF16)
    r_osbt = P.R()
    oT = SB("oT", [128, 8, T], BF16)
    r_oTa = P.R()
    r_oTb = P.R()

    pA = [PS("pA%d" % i, [128, 512], F32) for i in range(2)]
    r_pA = [P.RP() for _ in range(2)]
    pr = [PS("pr%d" % i, [128, 1024], F32) for i in range(2)]
    r_pr = [P.RP() for _ in range(2)]
    pai = [0]

    def nextA():
        b = pai[0] % 2
        pai[0] += 1
        return pA[b], r_pA[b]

    ev2 = rr(["act", "dve"])

    for i in range(ntiles):
        g = i % TPS
        x, rx = TN.norm(i)
        if i + 1 < ntiles:
            TN.load(src_t, i + 1)
        hT, r_hT = TN.hT, TN.r_hT
        rwl = float(os.environ.get("RW_LVL", "9"))
        if rwl < 0.5:
            continue
        P.op("dma", lambda e, i=i: e.dma_start(out=osbt[:], in_=osb_t[i]), writes=[r_osbt])
        if g == 0:
            P.op("pool", lambda e: e.memset(U[:, :, 1:2], 0.0), writes=[r_U])
        if rwl < 0.75:
            continue
        for f in range(14):
            pu, rpu = nextA()

            def proj(e, f=f, pu=pu):
                ins = None
                for k in range(8):
                    ins = e.matmul(pu[:, 0:T], lhsT=wrw[:, k, f * 128:(f + 1) * 128], rhs=hT[:, k, :], start=(k == 0), stop=(k == 7))
                return ins
            P.op("pe", proj, reads=[r_hT] + r_wrw, writes=[rpu])
            P.op("act", lambda e, f=f, pu=pu: e.activation(out=X[:, f, :], in_=pu[:, 0:T], func=AF.Copy, scale=omm[:, f:f + 1]), reads=[rpu, r_c], writes=[r_X[f]])
            if rwl < 0.85:
                continue
            P.op("dve", lambda e, f=f, pu=pu: e.tensor_copy(out=U[:, f, 2:T + 2], in_=pu[:, 0:T]), reads=[rpu], writes=[r_U])
            if rwl < 0.9:
                continue
            P.op("dve", lambda e, f=f: e.scalar_tensor_tensor(out=X[:, f, :], in0=U[:, f, 1:T + 1], scalar=mu[:, f:f + 1], in1=X[:, f, :], op0=ALU.mult, op1=ALU.add),
                 reads=[r_U, r_X[f], r_c], writes=[r_X[f]])
        if rwl >= 0.95:
            P.op("pool", lambda e: e.tensor_copy(out=U[:, :, 1:2], in_=U[:, :, T + 1:T + 2]), reads=[r_U], writes=[r_U])
        if rwl < 2:
            continue
        P.op("act", lambda e: e.activation(out=twxa[0:64, :], in_=X[0:64, 12, :], func=AF.Tanh), reads=[r_X[12]], writes=[r_twxa])
        P.op("dve", lambda e: e.tensor_copy(out=twxa[64:128, :], in_=X[64:128, 12, :]), reads=[r_X[12]], writes=[r_twxa])
        P.op("act", lambda e: e.activation(out=sgg[:], in_=X[:, 13, :], func=AF.Sigmoid), reads=[r_X[13]], writes=[r_sgg])
        for c in range(NCH):
            pg, rpg = nextA()
            P.op("pe", lambda e, c=c, pg=pg: e.matmul(pg[0:64, :], lhsT=sgg[:, c * 64:(c + 1) * 64], rhs=g2b[:], start=True, stop=True), reads=[r_sgg, r_c], writes=[rpg])
            P.op("act", lambda e, c=c, pg=pg: e.copy(out=Gt[:, c, :], in_=pg[0:64, :]), reads=[rpg], writes=[r_Gt[c]])
        if rwl < 3:
            continue
        def prep_hp(hp):
            xr, xk, xv = X[:, hp, :], X[:, 4 + hp, :], X[:, 8 + hp, :]
            rxr, rxk, rxv = r_X[hp], r_X[4 + hp], r_X[8 + hp]
            b0, b1, b2, b3, b4, b5 = bbs[hp % NSET]
            q0_, q1_, q2_, q3_, q4_, q5_ = r_bbs[hp % NSET]
            sqb, r_sqb, rkb, r_rkb = sqbs[hp % NSET], r_sqbs[hp % NSET], rkbs[hp % NSET], r_rkbs[hp % NSET]
            pw, rpw = nextA()
            yield
            P.op("pe", lambda e, hp=hp, pw=pw: e.matmul(pw[:, 0:T], lhsT=wa2[0:64, hp * 128:(hp + 1) * 128], rhs=twxa[0:64, :], start=True, stop=True), reads=[r_twxa, r_c], writes=[rpw])
            yield
            P.op("act", lambda e, hp=hp, pw=pw: e.activation(out=b0[:], in_=pw[:, 0:T], func=AF.Sigmoid, bias=w0c[:, hp:hp + 1]), reads=[rpw, r_c], writes=[q0_])
            pa_, rpa = nextA()
            yield
            P.op("pe", lambda e, hp=hp, pa_=pa_: e.matmul(pa_[:, 0:T], lhsT=wa2[64:128, hp * 128:(hp + 1) * 128], rhs=twxa[64:128, :], start=True, stop=True), reads=[r_twxa, r_c], writes=[rpa])
            yield
            P.op("act", lambda e, hp=hp, pa_=pa_: e.activation(out=b3[:], in_=pa_[:, 0:T], func=AF.Sigmoid, bias=a0c[:, hp:hp + 1]), reads=[rpa, r_c], writes=[q3_])
            yield
            P.op("dve", lambda e: e.tensor_tensor_scan(out=b1[:], data0=cm[:], data1=b0[:], initial=0.0, op0=ALU.mult, op1=ALU.add), reads=[q0_, r_c], writes=[q1_])
            yield
            P.op("pool", lambda e: e.tensor_tensor(out=b2[:], in0=b1[:], in1=b0[:], op=ALU.subtract), reads=[q0_, q1_], writes=[q2_])
            yield
            P.op("act", lambda e: e.activation(out=b0[:], in_=b1[:], func=AF.Exp, scale=-DEC), reads=[q1_, q2_], writes=[q0_])
            yield
            P.op("act", lambda e: e.activation(out=b1[:], in_=b1[:], func=AF.Exp, scale=DEC), reads=[q1_, q0_], writes=[q1_])
            yield
            P.op("act", lambda e: e.activation(out=b2[:], in_=b2[:], func=AF.Exp, scale=-DEC), reads=[q2_], writes=[q2_])
            yield
            P.op("act", lambda e, hp=hp, xk=xk: e.activation(out=b4[:], in_=xk, func=AF.Copy, scale=kkc[:, hp:hp + 1]), reads=[rxk, r_c], writes=[q4_])
            yield
            P.op("pool", lambda e: e.tensor_tensor(out=sqb[:], in0=b4[:], in1=b4[:], op=ALU.mult), reads=[q4_], writes=[r_sqb])
            pss, rpss = nextA()
            yield
            P.op("pe", lambda e, pss=pss: e.matmul(pss[:, 0:T], lhsT=BD[:], rhs=sqb[:], start=True, stop=True), reads=[r_sqb, r_c], writes=[rpss])
            yield
            P.op("act", lambda e, pss=pss: e.activation(out=b5[:], in_=pss[:, 0:T], func=AF.Sqrt), reads=[rpss], writes=[q5_])
            yield
            P.op("dve", lambda e: e.tensor_scalar(out=b5[:], in0=b5[:], scalar1=1e-12, scalar2=None, op0=ALU.max), reads=[q5_], writes=[q5_])
            yield
            P.op("dve", lambda e: e.reciprocal(out=b5[:], in_=b5[:]), reads=[q5_], writes=[q5_])
            yield
            P.op("dve", lambda e: e.tensor_tensor(out=b4[:], in0=b4[:], in1=b5[:], op=ALU.mult), reads=[q4_, q5_], writes=[q4_])
            yield
            P.op("dve", lambda e, hp=hp: e.tensor_scalar(out=b5[:], in0=b3[:], scalar1=kac[:, hp:hp + 1], scalar2=okac[:, hp:hp + 1], op0=ALU.mult, op1=ALU.add),
                 reads=[q3_, q4_, r_c], writes=[q5_])
            yield
            P.op("pool", lambda e, xk=xk: e.tensor_tensor(out=b5[:], in0=xk, in1=b5[:], op=ALU.mult), reads=[rxk, q5_], writes=[q5_])

            def v3(ap):
                return ap.rearrange("p (c t) -> p c t", t=64)
            A_, B_ = ARD[hp], BKI[hp]
            yield
            P.op("dve", lambda e, A_=A_: e.scalar_tensor_tensor(out=A_[:, :, 128:192], in0=v3(b4[:]), scalar=-1.0, in1=v3(b2[:]), op0=ALU.mult, op1=ALU.mult),
                 reads=[q4_, q2_], writes=[r_ARD[hp]])
            yield
            P.op("pool", lambda e, A_=A_, xr=xr: e.tensor_tensor(out=A_[:, :, 0:64], in0=v3(xr), in1=v3(b0[:]), op=ALU.mult), reads=[rxr, q0_], writes=[r_ARD[hp]])
            yield
            P.op("dve", lambda e, A_=A_: e.tensor_tensor(out=A_[:, :, 64:128], in0=I2[:].unsqueeze(1).to_broadcast([128, NCH, 64]),
                                                         in1=v3(b0[:])[:, :, 63:64].to_broadcast([128, NCH, 64]), op=ALU.mult), reads=[q0_, r_c], writes=[r_ARD[hp]])
            yield
            P.op("pool", lambda e: e.tensor_tensor(out=b2[:], in0=b4[:], in1=b3[:], op=ALU.mult), reads=[q4_, q3_, q2_], writes=[q2_])
            yield
            P.op("dve", lambda e, B_=B_: e.tensor_tensor(out=B_[:, :, 128:192], in0=v3(b2[:]), in1=v3(b1[:]), op=ALU.mult), reads=[q2_, q1_], writes=[r_BKI[hp]])
            yield
            P.op("pool", lambda e, B_=B_: e.tensor_tensor(out=B_[:, :, 0:64], in0=v3(b5[:]), in1=v3(b1[:]), op=ALU.mult), reads=[q5_, q1_], writes=[r_BKI[hp]])
            yield
            P.op("dve", lambda e, xr=xr: e.tensor_tensor(out=rkb[:], in0=xr, in1=b5[:], op=ALU.mult), reads=[rxr, q5_], writes=[r_rkb])
            pbn, rpbn = pr[1], r_pr[1]

            def bon(e, hp=hp, pbn=pbn):
                ins = None
                for c in range(NCH):
                    ins = e.matmul(pbn[0:64, c * 8 + hp * 2: c * 8 + hp * 2 + 2], lhsT=rkb[:, c * 64:(c + 1) * 64], rhs=RKbd[:, hp, :], start=True, stop=True)
                return ins
            yield
            P.op("pe", bon, reads=[r_rkb, r_c], writes=[rpbn])
            if hp == 3:
                P.op("act", lambda e, pbn=pbn: e.copy(out=bsc[:].rearrange("p c h -> p (c h)"), in_=pbn[0:64, 0:NCH * 8]), reads=[rpbn], writes=[r_bsc])
        for pair in ((0, 1), (2, 3)):
            gens = [prep_hp(h_) for h_ in pair]
            while gens:
                for g_ in list(gens):
                    try:
                        next(g_)
                    except StopIteration:
                        gens.remove(g_)
        if rwl < 4:
            continue
        for c in range(NCH):
            slot = (i * NCH + c) % NSV
            pv, rpv = nextA()

            def vt(e, c=c, pv=pv):
                ins = None
                for hp in range(4):
                    ins = e.transpose(out=pv[0:64, hp * 128:(hp + 1) * 128], in_=X[:, 8 + hp, c * 64:(c + 1) * 64], identity=C.ident_f[:])
                return ins
            P.op("pe", vt, reads=[r_X[8], r_X[9], r_X[10], r_X[11], C.r], writes=[rpv])
            P.op("act", lambda e, slot=slot, pv=pv: e.copy(out=SV[slot][0:64, :, :].rearrange("p h d -> p (h d)"), in_=pv[0:64, :]), reads=[rpv], writes=[r_SVv[slot]])
        if g == 0:
            slot0 = (i * NCH) % NSV
            P.op("pool", lambda e, slot0=slot0: e.memset(SV[slot0][64:128, :, :], 0.0), writes=[r_SVs[slot0]])
        if rwl < 4.05:
            continue
        for c in range(NCH):
            slot = (i * NCH + c) % NSV
            nslot = (i * NCH + c + 1) % NSV
            hgs = (0, 1)

            def hops(hg, h4):
                return h4, hg * 64

            for hg in hgs:
                def mma(e, hg=hg, c=c):
                    ins = None
                    for h4 in range(4):
                        hp, pb = hops(hg, h4)
                        ins = e.matmul(pr[hg][0:64, h4 * 256:h4 * 256 + 192], lhsT=ARD[hp][pb:pb + 64, c, 128:192], rhs=BKI[hp][pb:pb + 64, c, 0:192], start=True, stop=True)
                    return ins
                P.op("pe", mma, reads=r_ARD + r_BKI, writes=[r_pr[hg]])
                pv4 = pr[hg][0:64, :].rearrange("p (h w) -> p h w", w=256)
                P.op("dve", lambda e, hg=hg, pv4=pv4: e.tensor_tensor(out=ZPP[hg][0][:, :, 0:192], in0=pv4[:, :, 0:192], in1=mA[:].unsqueeze(1).to_broadcast([64, 4, 192]), op=ALU.mult),
                     reads=[r_pr[hg], r_c], writes=[r_Zb[hg][0], r_PP[hg][0]])
            for hg in hgs:
                def mmb(e, hg=hg, c=c):
                    ins = None
                    for h4 in range(4):
                        hp, pb = hops(hg, h4)
                        ins = e.matmul(pr[hg][0:64, h4 * 256:h4 * 256 + 192], lhsT=BKI[hp][pb:pb + 64, c, 128:192], rhs=ARD[hp][pb:pb + 64, c, 0:192], start=True, stop=True)
                    return ins
                P.op("pe", mmb, reads=r_ARD + r_BKI, writes=[r_pr[hg]])
                pv4 = pr[hg][0:64, :].rearrange("p (h w) -> p h w", w=256)
                P.op("dve", lambda e, hg=hg, pv4=pv4: e.tensor_tensor(out=BRB[hg][:], in0=pv4[:, :, 0:192], in1=mB[:].unsqueeze(1).to_broadcast([64, 4, 192]), op=ALU.mult),
                     reads=[r_pr[hg], r_c], writes=[r_BRB[hg]])
            for hg in hgs:
                def mmc0(e, hg=hg, c=c):
                    ins = None
                    for h4 in range(4):
                        hp, pb = hops(hg, h4)
                        ins = e.matmul(pr[hg][:, h4 * 256:h4 * 256 + 128], lhsT=BKI[hp][pb:pb + 64, c, 0:128], rhs=ARD[hp][pb:pb + 64, c, 0:128], start=True, stop=True)
                    return ins
                P.op("pe", mmc0, reads=r_ARD + r_BKI, writes=[r_pr[hg]])
                pv4f = pr[hg][:, :].rearrange("p (h w) -> p h w", w=256)
                P.op("dve", lambda e, hg=hg, pv4f=pv4f: e.tensor_tensor(out=LBa[hg][:], in0=pv4f[:, :, 0:128], in1=mC[:].unsqueeze(1).to_broadcast([128, 4, 128]), op=ALU.mult),
                     reads=[r_pr[hg], r_c], writes=[r_LBa[hg]])
            for lvl in range(6):
                cur, nxt = lvl % 2, (lvl + 1) % 2
                for hg in hgs:
                    def lv(e, hg=hg, cur=cur, lvl=lvl):
                        ins = None
                        for h4 in range(4):
                            zp = ZPP[hg][cur]
                            Pm = BRB[hg][:, h4, 128:192] if lvl == 0 else zp[:, h4, 192:256]
                            w = 192 if lvl < 5 else 128
                            ins = e.matmul(pr[hg][0:64, h4 * 256:h4 * 256 + w], lhsT=Pm, rhs=zp[:, h4, 0:w], start=True, stop=True)
                            if lvl < 5:
                                ins = e.matmul(pr[hg][0:64, h4 * 256 + 192:h4 * 256 + 256], lhsT=zp[:, h4, 128:192], rhs=Pm, start=True, stop=True)
                        return ins
                    P.op("pe", lv, reads=[r_Zb[hg][cur], r_PP[hg][cur]] + ([r_BRB[hg]] if lvl == 0 else []), writes=[r_pr[hg]])
                    pv4 = pr[hg][0:64, :].rearrange("p (h w) -> p h w", w=256)
                    if lvl < 5:
                        P.op("dve", lambda e, hg=hg, nxt=nxt, pv4=pv4: e.tensor_copy(out=ZPP[hg][nxt][:, :, 128:256], in_=pv4[:, :, 128:256]), reads=[r_pr[hg]], writes=[r_PP[hg][nxt]])
                    P.op("dve", lambda e, hg=hg, nxt=nxt, cur=cur, pv4=pv4: e.tensor_tensor(out=ZPP[hg][nxt][:, :, 0:128], in0=pv4[:, :, 0:128], in1=ZPP[hg][cur][:, :, 0:128], op=ALU.add),
                         reads=[r_pr[hg], r_Zb[hg][cur]], writes=[r_Zb[hg][nxt]])
            for hg in hgs:
                def mmc(e, hg=hg, c=c):
                    ins = None
                    for h4 in range(4):
                        o = pr[hg][:, h4 * 256:h4 * 256 + 128]
                        ins = e.matmul(o, lhsT=ZPP[hg][0][:, h4, 0:128], rhs=BRB[hg][:, h4, 0:128], start=True, stop=True)
                    return ins
                P.op("pe", mmc, reads=[r_Zb[hg][0], r_BRB[hg]], writes=[r_pr[hg]])
                pv4f = pr[hg][:, :].rearrange("p (h w) -> p h w", w=256)
                P.op("dve", lambda e, hg=hg, pv4f=pv4f: e.tensor_tensor(out=LB[hg][:], in0=pv4f[:, :, 0:128], in1=LBa[hg][:], op=ALU.add),
                     reads=[r_pr[hg], r_LBa[hg]], writes=[r_LB[hg]])
            if rwl < 4.5:
                continue
            pS, rpS = nextA()

            def seq(e, slot=slot, pS=pS):
                ins = None
                for h in range(8):
                    ins = e.matmul(pS[:, h * 64:(h + 1) * 64], lhsT=LB[h % 2][:, h // 2, :], rhs=SV[slot][:, h, :], start=True, stop=True)
                return ins
            P.op("pe", seq, reads=r_LB + [r_SVv[slot], r_SVs[slot]], writes=[rpS])
            P.op("act", lambda e, nslot=nslot, pS=pS: e.copy(out=SV[nslot][64:128, :, :].rearrange("p h d -> p (h d)"), in_=pS[64:128, :]), reads=[rpS], writes=[r_SVs[nslot]])
            P.op("dve", lambda e, pS=pS: e.tensor_copy(out=Yc[:], in_=pS[0:64, :]), reads=[rpS], writes=[r_Yc])
            if rwl < 6:
                continue
            Y3 = Yc[:].rearrange("p (h d) -> p h d", d=64)
            T13 = t1[:].rearrange("p (h d) -> p h d", d=64)
            T23 = t2[:].rearrange("p (h d) -> p h d", d=64)
            V3 = SV[slot][0:64, :, :]
            P.op("act", lambda e: e.activation(out=t1[:], in_=Yc[:], func=AF.Square), reads=[r_Yc], writes=[r_t1])
            P.op("dve", lambda e, Y3=Y3: e.tensor_reduce(out=st[:, 0:8], in_=Y3, axis=AX.X, op=ALU.add), reads=[r_Yc], writes=[r_st])
            P.op("dve", lambda e, T13=T13: e.tensor_reduce(out=st[:, 8:16], in_=T13, axis=AX.X, op=ALU.add), reads=[r_t1], writes=[r_st])
            P.op("dve", lambda e: e.tensor_scalar(out=st[:, 16:24], in0=st[:, 0:8], scalar1=1.0 / 64, scalar2=None, op0=ALU.mult), reads=[r_st], writes=[r_st])
            P.op("dve", lambda e: e.tensor_tensor(out=st[:, 24:32], in0=st[:, 16:24], in1=st[:, 16:24], op=ALU.mult), reads=[r_st], writes=[r_st])
            P.op("dve", lambda e: e.scalar_tensor_tensor(out=st[:, 32:40], in0=st[:, 8:16], scalar=1.0 / 64, in1=st[:, 24:32], op0=ALU.mult, op1=ALU.subtract),
                 reads=[r_st], writes=[r_st])
            P.op("act", lambda e: e.activation(out=st[:, 40:48], in_=st[:, 32:40], func=AF.Sqrt, bias=LNX_EPS), reads=[r_st], writes=[r_st])
            P.op("dve", lambda e: e.reciprocal(out=st[:, 40:48], in_=st[:, 40:48]), reads=[r_st], writes=[r_st])
            P.op("pool", lambda e, Y3=Y3, T13=T13: e.tensor_tensor(out=T13, in0=Y3, in1=st[:, 16:24].unsqueeze(2).to_broadcast([64, 8, 64]), op=ALU.subtract),
                 reads=[r_Yc, r_st, r_t1], writes=[r_t1])
            P.op("pool", lambda e, T13=T13: e.tensor_tensor(out=T13, in0=T13, in1=st[:, 40:48].unsqueeze(2).to_broadcast([64, 8, 64]), op=ALU.mult),
                 reads=[r_st, r_t1], writes=[r_t1])
            P.op("pool", lambda e: e.tensor_tensor(out=t1[:], in0=t1[:], in1=lnw_b[:], op=ALU.mult), reads=[r_t1, r_c], writes=[r_t1])
            P.op("pool", lambda e: e.tensor_tensor(out=t1[:], in0=t1[:], in1=lnb_b[:], op=ALU.add), reads=[r_t1, r_c], writes=[r_t1])
            P.op("pool", lambda e, c=c, V3=V3, T23=T23: e.tensor_tensor(out=T23, in0=V3, in1=bsc[:, c, :].unsqueeze(2).to_broadcast([64, 8, 64]), op=ALU.mult),
                 reads=[r_SVv[slot], r_bsc], writes=[r_t2])
            P.op("pool", lambda e: e.tensor_tensor(out=t1[:], in0=t1[:], in1=t2[:], op=ALU.add), reads=[r_t1, r_t2], writes=[r_t1])
            P.op("pool", lambda e, c=c: e.tensor_tensor(out=orw[:, c, :], in0=t1[:], in1=Gt[:, c, :], op=ALU.mult), reads=[r_t1, r_Gt[c]], writes=[r_orw])
        if rwl < 7:
            continue
        ptb, rptb = TN.ptr[0], TN.r_ptr[0]

        def tro(e, ptb=ptb):
            ins = None
            for s in range(NS):
                for hp in range(4):
                    ins = e.transpose(out=ptb[:, hp * T + s * 128: hp * T + (s + 1) * 128], in_=osbt[:, s, hp * 128:(hp + 1) * 128], identity=C.ident_bf[:])
            return ins
        P.op("pe", tro, reads=[r_osbt, C.r], writes=[rptb])
        P.op("act", lambda e, ptb=ptb: e.copy(out=oT[:, 0:4, :], in_=ptb[:, 0:4 * T].rearrange("p (h t) -> p h t", h=4)), reads=[rptb], writes=[r_oTa])
        ptc, rptc = TN.ptr[1], TN.r_ptr[1]

        def trr(e, ptc=ptc):
            ins = None
            for c in range(NCH):
                for hp in range(4):
                    ins = e.transpose(out=ptc[:, hp * T + c * 64: hp * T + (c + 1) * 64], in_=orw[:, c, hp * 128:(hp + 1) * 128], identity=C.ident_bf[0:64, 0:64])
            return ins
        P.op("pe", trr, reads=[r_orw, C.r], writes=[rptc])
        P.op("dve", lambda e, ptc=ptc: e.tensor_copy(out=oT[:, 4:8, :], in_=ptc[:, 0:4 * T].rearrange("p (h t) -> p h t", h=4)), reads=[rptc], writes=[r_oTb])
        for s in range(NS):
            for hf in range(2):
                po, rpo = nextA()

                def wo(e, s=s, hf=hf, po=po):
                    ins = None
                    for f in range(8):
                        ins = e.matmul(po[:], lhsT=oT[:, f, s * 128:(s + 1) * 128], rhs=wout[:, f, hf * 512:(hf + 1) * 512], start=(f == 0), stop=(f == 7))
                    return ins
                P.op("pe", wo, reads=[r_oTa, r_oTb] + r_wout, writes=[rpo])
                P.op("dve", lambda e, s=s, hf=hf, po=po, x=x: e.tensor_tensor(out=x[:, s, hf * 512:(hf + 1) * 512], in0=po[:], in1=x[:, s, hf * 512:(hf + 1) * 512], op=ALU.add),
                     reads=[rpo, rx], writes=[rx])
        P.op("dma", lambda e, i=i, x=x: e.dma_start(out=dst_t[i], in_=x[:]), reads=[rx])
        if DEBUG and i == 0:
            for nm, t, rs, shp, dt in [("X", X, r_X, [128, 14 * T], F32), ("orw", orw, [r_orw], [64, NCH * 512], BF16), ("Gt", Gt, r_Gt, [64, NCH * 512], F32),
                                       ("bsc", bsc, [r_bsc], [64, NCH * 8], F32)]:
                o = nc.dram_tensor("dbg_" + nm, shp, F32, kind="ExternalOutput").ap()
                fl = t[:].rearrange("p a b -> p (a b)")
                if dt == BF16:
                    tmpf = SB("dbgf_" + nm, shp, F32)
                    P.op("dve", lambda e, tmpf=tmpf, fl=fl: e.tensor_copy(out=tmpf[:], in_=fl), reads=rs, writes=[r_c])
                    P.op("dma", lambda e, o=o, tmpf=tmpf: e.dma_start(out=o[:, :], in_=tmpf[:]), reads=[r_c])
                else:
                    P.op("dma", lambda e, o=o, fl=fl: e.dma_start(out=o[:, :], in_=fl), reads=rs)
    n = P.emit()
    es.close()
    return n


W_NAMES = ["norm_ffn1", "ffn1_gate", "ffn1_up", "ffn1_down", "norm_mix", "w_in", "sb_q_norm", "sb_k_norm", "sb_out_norm",
           "rw_mu", "rw_w0", "rw_w2", "rw_a0", "rw_a2", "rw_g2", "rw_k_k", "rw_k_a", "rw_r_k", "rw_ln_w", "rw_ln_b",
           "w_out", "norm_ffn2", "ffn2_gate", "ffn2_up", "ffn2_down"]
W_SHAPES = {"norm_ffn1": [D], "ffn1_gate": [D, DFF], "ffn1_up": [D, DFF], "ffn1_down": [DFF, D], "norm_mix": [D], "w_in": [D, PROJ],
            "sb_q_norm": [64], "sb_k_norm": [64], "sb_out_norm": [512], "rw_mu": [1792], "rw_w0": [512], "rw_w2": [64, 512],
            "rw_a0": [512], "rw_a2": [64, 512], "rw_g2": [128, 512], "rw_k_k": [512], "rw_k_a": [512], "rw_r_k": [512],
            "rw_ln_w": [512], "rw_ln_b": [512], "w_out": [D, D], "norm_ffn2": [D], "ffn2_gate": [D, DFF], "ffn2_up": [D, DFF],
            "ffn2_down": [DFF, D]}


def build(phases=("ffn1", "attn", "rwkv", "ffn2"), ntok=NTOK):
    nc = bass.Bass("TRN2", target_bir_lowering=False)
    x = nc.dram_tensor("x", [ntok, D], F32, kind="ExternalInput").ap()
    W = {n: nc.dram_tensor(n, W_SHAPES[n], F32, kind="ExternalInput").ap() for n in W_NAMES}
    out = nc.dram_tensor("out", [ntok, D], F32, kind="ExternalOutput").ap()
    s1 = nc.dram_tensor("scr1", [ntok, D], F32, kind="Internal").ap()
    s2 = nc.dram_tensor("scr2", [ntok, D], F32, kind="Internal").ap()
    osb = nc.dram_tensor("osb", [ntok, 512], BF16, kind="ExternalOutput" if (DEBUG and not os.environ.get("OSB_INT")) else "Internal").ap()
    es = ExitStack()
    P = Prog(nc, es)
    C = make_consts(P, nc, es)
    init_consts(P, C)
    chain = list(phases)
    cur = x
    for pi, ph in enumerate(chain):
        lastp = pi == max(j for j, p_ in enumerate(chain) if p_ != "attn")
        dst = out if lastp else (s1 if cur is not s1 else s2)
        if ph == "attn":
            attn_phase(P, nc, C, "at", cur, osb, W, ntok)
            continue
        if ph == "rwkv":
            rwkv_phase(P, nc, C, "rw", cur, dst, osb, W, ntok)
            cur = dst
            continue
        if ph == "ffn1":
            ffn_phase(P, nc, C, "f1", cur, dst, W["norm_ffn1"], W["ffn1_gate"], W["ffn1_up"], W["ffn1_down"], ntok)
        elif ph == "ffn2":
            ffn_phase(P, nc, C, "f2", cur, dst, W["norm_ffn2"], W["ffn2_gate"], W["ffn2_up"], W["ffn2_down"], ntok)
        else:
            raise NotImplementedError(ph)
        cur = dst
    P.op("pool", lambda e: e.memset(C.ones_f[:, 0:1], 1.0), writes=[C.r])
    P.emit(final=True)
    es.close()
    return nc


_CACHE = {}


def kernel(**inputs):
    x = np.ascontiguousarray(np.asarray(inputs["x"], dtype=np.float32))
    B = x.shape[0]
    ncores = 8
    per = B // ncores
    key = "full"
    if key not in _CACHE:
        _CACHE[key] = build()
    nc = _CACHE[key]
    wmap = {n: np.ascontiguousarray(np.asarray(inputs[n], dtype=np.float32).reshape(W_SHAPES[n])) for n in W_NAMES}
    in_maps = []
    for c in range(ncores):
        m = {"x": x[c * per:(c + 1) * per].reshape(per * SEQ, D)}
        m.update(wmap)
        in_maps.append(m)
    res = run_bass_kernel_spmd(nc, in_maps, core_ids=list(range(ncores)))
    outs = [np.asarray(r["out"]).reshape(per, SEQ, D) for r in res.results]
    return np.concatenate(outs, axis=0).astype(np.float32)
```
